# Optimizing a Trainium2 kernel written in Bass

```python
import numpy as np
import jax
import jax.numpy as jnp
from jax import lax

D_MODEL = 1024
BATCH = 32
SEQ = 2048
DEPTH = 1

POOL_WINDOWS = (2, 4, 8, 16)
POOL_GROUPS = len(POOL_WINDOWS)
POOL_GROUP_W = 128
POOL_W = POOL_GROUPS * POOL_GROUP_W
N_HEADS = 8
HEAD_DIM = 64
N_KV_GROUPS = 2
HEADS_PER_GROUP = N_HEADS // N_KV_GROUPS
ATTN_W = N_HEADS * HEAD_DIM
KV_W = N_KV_GROUPS * HEAD_DIM
N_KV_SLOTS = 6
N_NSA_BRANCHES = 3
CMP_BLOCK = 32
CMP_STRIDE = 16
CMP_HIDDEN = 128
SEL_BLOCK = 64
N_SELECT = 16
WINDOW = 512
QUERY_CHUNK = 16
ROPE_THETA = 500000.0
ROT_DIM = HEAD_DIM // 4
N_MERGE = 2
D_FF = 2816
CONV_WIDTH = 3
IN_W = POOL_W + ATTN_W + N_KV_SLOTS * KV_W + N_HEADS * N_NSA_BRANCHES + N_MERGE * D_MODEL
RMS_EPS = 1e-6
NEG_INF = -1e30
SEL_FORCE = 1e9

kernel_name = "hybrid_pool_nsa_convffn"


def rms_norm(x, g):
    xf = x.astype(jnp.float32)
    y = xf * lax.rsqrt(jnp.mean(xf * xf, axis=-1, keepdims=True) + RMS_EPS)
    return (y * g.astype(jnp.float32)).astype(x.dtype)


def partial_rope(x, pos):
    half = ROT_DIM // 2
    inv = ROPE_THETA ** (-(jnp.arange(half, dtype=jnp.float32) * 2.0 / ROT_DIM))
    ang = pos.astype(jnp.float32)[:, :, None, None] * inv
    cos, sin = jnp.cos(ang), jnp.sin(ang)
    xf = x.astype(jnp.float32)
    x1, x2 = xf[..., :half], xf[..., half:ROT_DIM]
    rot = jnp.concatenate([x1 * cos - x2 * sin, x2 * cos + x1 * sin], axis=-1).astype(x.dtype)
    return jnp.concatenate([rot, x[..., ROT_DIM:]], axis=-1)


def masked_softmax(s, mask, axis):
    s = jnp.where(mask, s.astype(jnp.float32), NEG_INF)
    p = jax.nn.softmax(s, axis=axis)
    return jnp.where(mask, p, 0.0)


def pooling_mixer(u, pool_w, pool_scale):
    S = u.shape[1]
    uf = u.astype(jnp.float32)
    cs = jnp.cumsum(uf, axis=1)
    count = jnp.arange(S, dtype=jnp.float32) + 1.0
    outs = []
    for gi, w in enumerate(POOL_WINDOWS):
        sl = slice(gi * POOL_GROUP_W, (gi + 1) * POOL_GROUP_W)
        c = cs[..., sl]
        lag = jnp.pad(c[:, :S - w], ((0, 0), (w, 0), (0, 0)))
        mean = (c - lag) / jnp.minimum(count, float(w))[None, :, None]
        pooled = (mean - uf[..., sl]).astype(u.dtype)
        outs.append(pooled @ pool_w[gi])
    return jnp.concatenate(outs, axis=-1) * pool_scale


def compress_blocks(kv, pe, w1, w2, idx):
    blk = kv[:, idx] + pe[None, None, :, None, :]
    hid = jax.nn.silu(jnp.einsum('bnlgd,ldh->bngh', blk, w1))
    return jnp.einsum('bngh,hd->bngd', hid, w2)


def nsa_mixer(q, k_cmp, v_cmp, k_sel, v_sel, k_win, v_win, gates, positions,
              q_norm_g, k_norm_g, cmp_pe, cmp_w1, cmp_w2):
    B, S = q.shape[0], q.shape[1]
    G, R, dh = N_KV_GROUPS, HEADS_PER_GROUP, HEAD_DIM
    scale = HEAD_DIM ** -0.5
    q = partial_rope(rms_norm(q, q_norm_g), positions)

    n_cmp = (S - CMP_BLOCK) // CMP_STRIDE + 1
    cmp_idx = np.arange(n_cmp)[:, None] * CMP_STRIDE + np.arange(CMP_BLOCK)[None, :]
    cmp_end = jnp.asarray(cmp_idx[:, -1], dtype=jnp.int32)
    kc = compress_blocks(k_cmp, cmp_pe[0], cmp_w1[0], cmp_w2[0], cmp_idx)
    vc = compress_blocks(v_cmp, cmp_pe[1], cmp_w1[1], cmp_w2[1], cmp_idx)
    kc = partial_rope(rms_norm(kc, k_norm_g[0]), positions[:, cmp_idx[:, -1]])

    n_sel = S // SEL_BLOCK
    k_top = min(N_SELECT, n_sel)
    ks = partial_rope(rms_norm(k_sel, k_norm_g[1]), positions)
    ks_blocks = ks.reshape(B, n_sel, SEL_BLOCK, G, dh).transpose(0, 3, 1, 2, 4)
    vs_blocks = v_sel.reshape(B, n_sel, SEL_BLOCK, G, dh).transpose(0, 3, 1, 2, 4)
    sel_start = np.arange(n_sel)[:, None] * SEL_BLOCK
    cmp_start = np.arange(n_cmp)[None, :] * CMP_STRIDE
    ov = np.clip(np.minimum(sel_start + SEL_BLOCK, cmp_start + CMP_BLOCK)
                 - np.maximum(sel_start, cmp_start), 0, None) / CMP_BLOCK
    overlap = jnp.asarray(ov, dtype=jnp.float32)
    gather_blocks = jax.vmap(jax.vmap(lambda blocks, i: blocks[i]))

    kw = partial_rope(rms_norm(k_win, k_norm_g[2]), positions)
    kw_pad = jnp.pad(kw, ((0, 0), (WINDOW, 0), (0, 0), (0, 0)))
    vw_pad = jnp.pad(v_win, ((0, 0), (WINDOW, 0), (0, 0), (0, 0)))

    n_chunks = S // QUERY_CHUNK
    q_ch = q.reshape(B, n_chunks, QUERY_CHUNK, G, R, dh).swapaxes(0, 1)
    g_ch = gates.reshape(B, n_chunks, QUERY_CHUNK, G, R, N_NSA_BRANCHES).swapaxes(0, 1)
    j_blk = jnp.arange(n_sel)
    tok_off = jnp.arange(SEL_BLOCK)
    win_off = jnp.arange(WINDOW + QUERY_CHUNK)

    def chunk(args):
        c, qc, gc = args
        t = c * QUERY_CHUNK + jnp.arange(QUERY_CHUNK)
        s = jnp.einsum('bqgrd,bngd->bgrqn', qc, kc) * scale
        p_cmp = masked_softmax(s, cmp_end[None, :] <= t[:, None], -1)
        o_cmp = jnp.einsum('bgrqn,bngd->bqgrd', p_cmp.astype(vc.dtype), vc)
        imp = jnp.einsum('bgrqn,jn->bgqj', p_cmp, overlap)
        cur = t // SEL_BLOCK
        forced = (j_blk[None, :] == 0) | (j_blk[None, :] == cur[:, None]) | (j_blk[None, :] == cur[:, None] - 1)
        future = j_blk[None, :] * SEL_BLOCK > t[:, None]
        imp = jnp.where(forced, SEL_FORCE, jnp.where(future, NEG_INF, imp))
        _, top = lax.top_k(imp, k_top)
        ksel = gather_blocks(ks_blocks, top)
        vsel = gather_blocks(vs_blocks, top)
        tok = top[..., None] * SEL_BLOCK + tok_off
        m_sel = (tok <= t[None, None, :, None, None])[:, :, None]
        s = jnp.einsum('bqgrd,bgqkld->bgrqkl', qc, ksel) * scale
        p = masked_softmax(s, m_sel, (-2, -1))
        o_sel = jnp.einsum('bgrqkl,bgqkld->bqgrd', p.astype(vsel.dtype), vsel)
        kwc = lax.dynamic_slice_in_dim(kw_pad, c * QUERY_CHUNK, WINDOW + QUERY_CHUNK, axis=1)
        vwc = lax.dynamic_slice_in_dim(vw_pad, c * QUERY_CHUNK, WINDOW + QUERY_CHUNK, axis=1)
        kpos = c * QUERY_CHUNK - WINDOW + win_off
        m_win = (kpos[None, :] >= 0) & (kpos[None, :] <= t[:, None]) & (t[:, None] - kpos[None, :] < WINDOW)
        s = jnp.einsum('bqgrd,bkgd->bgrqk', qc, kwc) * scale
        p = masked_softmax(s, m_win, -1)
        o_win = jnp.einsum('bgrqk,bkgd->bqgrd', p.astype(vwc.dtype), vwc)
        return gc[..., 0:1] * o_cmp + gc[..., 1:2] * o_sel + gc[..., 2:3] * o_win

    out = lax.map(chunk, (jnp.arange(n_chunks), q_ch, g_ch))
    return out.swapaxes(0, 1).reshape(B, S, ATTN_W)


def causal_dwconv(x, w, b):
    C = x.shape[-1]
    y = lax.conv_general_dilated(x, w[:, None, :].astype(x.dtype), window_strides=(1,),
                                 padding=[(CONV_WIDTH - 1, 0)],
                                 dimension_numbers=('NWC', 'WIO', 'NWC'),
                                 feature_group_count=C)
    return y + b


def setup_inputs(seed: int = 0) -> dict:
    key = jax.random.key(seed)
    ks = jax.random.split(key, 20)
    f32 = jnp.float32

    def nrm(k, shape, s):
        return jax.random.normal(k, shape, f32) * s

    x = jax.random.normal(ks[0], (BATCH, SEQ, D_MODEL), f32)
    offsets = jax.random.randint(ks[1], (BATCH, 1), 0, 4096, dtype=jnp.int32)
    positions = (jnp.arange(SEQ, dtype=jnp.int32)[None, :] + offsets).astype(jnp.int32)
    return {
        "x": x,
        "positions": positions,
        "mix_norm_g": 1.0 + nrm(ks[2], (DEPTH, D_MODEL), 0.05),
        "w_in": nrm(ks[3], (DEPTH, D_MODEL, IN_W), D_MODEL ** -0.5),
        "q_norm_g": 1.0 + nrm(ks[4], (DEPTH, HEAD_DIM), 0.05),
        "k_norm_g": 1.0 + nrm(ks[5], (DEPTH, 3, HEAD_DIM), 0.05),
        "cmp_pe": nrm(ks[6], (DEPTH, 2, CMP_BLOCK, HEAD_DIM), 0.1),
        "cmp_w1": nrm(ks[7], (DEPTH, 2, CMP_BLOCK, HEAD_DIM, CMP_HIDDEN), (CMP_BLOCK * HEAD_DIM) ** -0.5),
        "cmp_w2": nrm(ks[8], (DEPTH, 2, CMP_HIDDEN, HEAD_DIM), CMP_HIDDEN ** -0.5),
        "pool_w": nrm(ks[9], (DEPTH, POOL_GROUPS, POOL_GROUP_W, POOL_GROUP_W), POOL_GROUP_W ** -0.5),
        "pool_scale": 1.0 + nrm(ks[10], (DEPTH, POOL_W), 0.1),
        "w_pool_out": nrm(ks[11], (DEPTH, POOL_W, D_MODEL), POOL_W ** -0.5),
        "w_nsa_out": nrm(ks[12], (DEPTH, ATTN_W, D_MODEL), ATTN_W ** -0.5),
        "w_out": nrm(ks[13], (DEPTH, D_MODEL, D_MODEL), D_MODEL ** -0.5),
        "ffn_norm_g": 1.0 + nrm(ks[14], (DEPTH, D_MODEL), 0.05),
        "w_up": nrm(ks[15], (DEPTH, D_MODEL, 2 * D_FF), D_MODEL ** -0.5),
        "conv_w": nrm(ks[16], (DEPTH, CONV_WIDTH, D_FF), CONV_WIDTH ** -0.5),
        "conv_b": nrm(ks[17], (DEPTH, D_FF), 0.01),
        "w_down": nrm(ks[18], (DEPTH, D_FF, D_MODEL), D_FF ** -0.5),
    }


def reference(x, positions, mix_norm_g, w_in, q_norm_g, k_norm_g, cmp_pe, cmp_w1, cmp_w2,
              pool_w, pool_scale, w_pool_out, w_nsa_out, w_out, ffn_norm_g, w_up, conv_w,
              conv_b, w_down):
    B, S = x.shape[0], x.shape[1]
    split_at = [POOL_W, POOL_W + ATTN_W, POOL_W + ATTN_W + N_KV_SLOTS * KV_W,
                POOL_W + ATTN_W + N_KV_SLOTS * KV_W + N_HEADS * N_NSA_BRANCHES]
    for l in range(DEPTH):
        h = rms_norm(x, mix_norm_g[l])
        proj = h @ w_in[l]
        u, q, kv, nsa_g, merge_g = jnp.split(proj, split_at, axis=-1)
        q = q.reshape(B, S, N_HEADS, HEAD_DIM)
        kv = kv.reshape(B, S, N_KV_SLOTS, N_KV_GROUPS, HEAD_DIM)
        nsa_gates = jax.nn.sigmoid(nsa_g.reshape(B, S, N_HEADS, N_NSA_BRANCHES))
        gate_pool, gate_nsa = jnp.split(jax.nn.sigmoid(merge_g), N_MERGE, axis=-1)
        y_pool = pooling_mixer(u, pool_w[l], pool_scale[l]) @ w_pool_out[l]
        y_nsa = nsa_mixer(q, kv[:, :, 0], kv[:, :, 1], kv[:, :, 2], kv[:, :, 3], kv[:, :, 4], kv[:, :, 5],
                          nsa_gates, positions, q_norm_g[l], k_norm_g[l], cmp_pe[l], cmp_w1[l],
                          cmp_w2[l]) @ w_nsa_out[l]
        x = x + (gate_pool * y_pool + gate_nsa * y_nsa) @ w_out[l]
        h = rms_norm(x, ffn_norm_g[l])
        gate_pre, val = jnp.split(h @ w_up[l], 2, axis=-1)
        gate_c = causal_dwconv(gate_pre, conv_w[l], conv_b[l])
        x = x + (jax.nn.silu(gate_c) * val) @ w_down[l]
    return x
```

```python
import numpy as np
import concourse.bass as bass
import concourse.mybir as mybir

F32 = mybir.dt.float32
BF16 = mybir.dt.bfloat16
I32 = mybir.dt.int32
AF = mybir.ActivationFunctionType
ALU = mybir.AluOpType
AX = mybir.AxisListType

_ESZ = {F32: 4, BF16: 2, I32: 4}


def esize(dt):
    if dt in _ESZ:
        return _ESZ[dt]
    s = str(dt)
    if '64' in s:
        return 8
    if '32' in s:
        return 4
    if '16' in s:
        return 2
    return 1


def bbox(ap):
    dims = ap.ap
    es = esize(ap.dtype)
    name = ap.tensor.name
    if name.startswith("pb"):
        return (name, 0, 128, 0, 2048)
    off = int(ap.offset)
    pstep, pcnt = dims[0]
    if pstep == 0:
        p0 = 0
        col = off
        pcnt_eff = 1
    else:
        p0 = off // pstep
        col = off % pstep
        pcnt_eff = pcnt
    ext = 0
    for st, cnt in dims[1:]:
        ext += abs(st) * (cnt - 1)
    return (name, p0, p0 + pcnt_eff, col * es, (col + ext + 1) * es)


class Op:
    __slots__ = ("eng", "fn", "deps", "dma_key", "idx", "marked", "mval", "waits", "dma_cnt")

    def __init__(self, eng, fn, dma_key):
        self.eng = eng
        self.fn = fn
        self.deps = set()
        self.dma_key = dma_key
        self.marked = False
        self.mval = 0
        self.waits = None
        self.dma_cnt = 0


ENGS = ("pe", "act", "dve", "pool", "sp")


class Prog:
    def __init__(self, nc):
        self.nc = nc
        self.ops = []
        self.live = {}
        self.dma_count = {}
        self.out_keys = set()
        self.notrack = set()

    def add(self, eng, fn, reads=(), writes=(), dma_key=None, is_out=False):
        op = Op(eng, fn, dma_key)
        op.idx = len(self.ops)
        self.ops.append(op)
        if dma_key is not None:
            c = self.dma_count.get(dma_key, 0) + 1
            self.dma_count[dma_key] = c
            op.dma_cnt = c
            if is_out:
                self.out_keys.add(dma_key)
        for ap in reads:
            self._access(op, ap, False)
        for ap in writes:
            self._access(op, ap, True)
        return op

    def _is_dma(self, op):
        return op.dma_key is not None

    def _access(self, op, ap, is_write):
        name, p0, p1, b0, b1 = bbox(ap)
        if name in self.notrack:
            return
        L = self.live.get(name)
        if L is None:
            L = []
            self.live[name] = L
        ops = self.ops
        newL = []
        for a in L:
            ap0, ap1, ab0, ab1, aw, aidx = a
            if aidx == op.idx:
                newL.append(a)
                continue
            if ap0 < p1 and p0 < ap1 and ab0 < b1 and b0 < ab1:
                prod = ops[aidx]
                if is_write or aw:
                    same = (prod.eng == op.eng) and not self._is_dma(prod) and not self._is_dma(op)
                    if same:
                        if op.eng != "pe":
                            op.deps.add(aidx)
                    else:
                        op.deps.add(aidx)
                contained = (p0 <= ap0 and ap1 <= p1 and b0 <= ab0 and ab1 <= b1)
                if contained:
                    if is_write:
                        continue
                    if (not aw) and prod.eng == op.eng and not self._is_dma(prod) and not self._is_dma(op):
                        continue
            newL.append(a)
        newL.append((p0, p1, b0, b1, is_write, op.idx))
        if len(newL) > 48:
            newL = self._compact(newL)
        self.live[name] = newL

    def _compact(self, L):
        ops = self.ops
        keep = []
        merged = {}
        for a in L:
            p0, p1, b0, b1, w, idx = a
            o = ops[idx]
            if w or self._is_dma(o):
                keep.append(a)
                continue
            m = merged.get(o.eng)
            if m is None:
                merged[o.eng] = [p0, p1, b0, b1, False, idx]
            else:
                m[0] = min(m[0], p0); m[1] = max(m[1], p1)
                m[2] = min(m[2], b0); m[3] = max(m[3], b1)
                m[5] = max(m[5], idx)
        for m in merged.values():
            keep.append(tuple(m))
        return keep

    def emit(self):
        nc = self.nc
        ops = self.ops
        for op in ops:
            for d in op.deps:
                ops[d].marked = True
        cnt = {e: 0 for e in ENGS}
        for op in ops:
            if op.dma_key is None and op.marked:
                cnt[op.eng] += 1
                op.mval = cnt[op.eng]
        seen = {e: {} for e in ENGS}
        dma_running = {}
        for op in ops:
            need = {}
            for d in op.deps:
                p = ops[d]
                if p.dma_key is not None:
                    key = ("dma", p.dma_key)
                    val = 16 * dma_running[p.dma_key]
                else:
                    key = ("eng", p.eng)
                    val = p.mval
                if need.get(key, 0) < val:
                    need[key] = val
            s = seen[op.eng]
            w = []
            for key, val in need.items():
                if s.get(key, 0) < val:
                    s[key] = val
                    w.append((key, val))
            op.waits = w
            if op.dma_key is not None:
                dma_running[op.dma_key] = op.dma_cnt
        import contextlib
        stack = contextlib.ExitStack()
        sems = {}
        for e in ENGS:
            sems[("eng", e)] = stack.enter_context(nc.semaphore("s_" + e))
        for k in self.dma_count:
            sems[("dma", k)] = stack.enter_context(nc.semaphore("d_" + str(k)))
        per_eng = {e: [op for op in ops if op.eng == e] for e in ENGS}
        finals = [(("dma", k), 16 * self.dma_count[k]) for k in self.out_keys]

        def run(engobj, elist, final=False):
            for op in elist:
                for key, val in op.waits:
                    engobj.wait_ge(sems[key], val)
                ins = op.fn(engobj)
                if op.dma_key is not None:
                    ins.then_inc(sems[("dma", op.dma_key)], 16)
                elif op.marked:
                    ins.then_inc(sems[("eng", op.eng)], 1)
            if final:
                for key, val in finals:
                    engobj.wait_ge(sems[key], val)

        with stack:
            with nc.Block() as block:
                @block.tensor
                def _(e):
                    run(e, per_eng["pe"])

                @block.scalar
                def _(e):
                    run(e, per_eng["act"])

                @block.vector
                def _(e):
                    run(e, per_eng["dve"])

                @block.gpsimd
                def _(e):
                    run(e, per_eng["pool"])

                @block.sync
                def _(e):
                    run(e, per_eng["sp"], final=True)
        self.stats = {e: len(per_eng[e]) for e in ENGS}
        self.stats["sem_max"] = dict(cnt)
        self.stats["nsem"] = len(sems)
        return nc

    def dma(self, out, in_, key, eng="sp", is_out=False, **kw):
        return self.add(eng, lambda e: e.dma_start(out=out, in_=in_, **kw),
                        reads=[in_], writes=[out], dma_key=key, is_out=is_out)

    def mm(self, out, lhsT, rhs, start=True, stop=True, **kw):
        return self.add("pe", lambda e: e.matmul(out, lhsT, rhs, start=start, stop=stop, **kw),
                        reads=[lhsT, rhs], writes=[out])

    def transpose(self, out, in_, ident):
        return self.add("pe", lambda e: e.transpose(out, in_, ident),
                        reads=[in_, ident], writes=[out])

    def act(self, out, in_, func, bias=None, scale=None, accum_out=None, eng="act"):
        reads = [in_]
        kw = {}
        if bias is not None:
            kw["bias"] = bias
            if not isinstance(bias, (int, float)):
                reads.append(bias)
        if scale is not None:
            kw["scale"] = scale
            if not isinstance(scale, (int, float)):
                reads.append(scale)
        writes = [out]
        if accum_out is not None:
            kw["accum_out"] = accum_out
            writes.append(accum_out)
        return self.add(eng, lambda e: e.activation(out, in_, func, **kw), reads=reads, writes=writes)

    def tt(self, out, in0, in1, op, eng="dve"):
        return self.add(eng, lambda e: e.tensor_tensor(out, in0, in1, op), reads=[in0, in1], writes=[out])

    def ts(self, out, in0, s1, op0, s2=None, op1=None, eng="dve", accum_out=None):
        reads = [in0]
        if not isinstance(s1, (int, float)):
            reads.append(s1)
        if s2 is not None and not isinstance(s2, (int, float)):
            reads.append(s2)
        writes = [out]
        kw = {}
        if accum_out is not None:
            kw["accum_out"] = accum_out
            writes.append(accum_out)
        if op1 is None:
            if accum_out is None:
                return self.add(eng, lambda e: e.tensor_single_scalar(out, in0, s1, op0), reads=reads, writes=writes)
            return self.add(eng, lambda e: e.tensor_scalar(out, in0, s1, None, op0, **kw), reads=reads, writes=writes)
        return self.add(eng, lambda e: e.tensor_scalar(out, in0, s1, s2, op0, op1, **kw), reads=reads, writes=writes)

    def stt(self, out, in0, scalar, in1, op0, op1, eng="dve"):
        reads = [in0, in1]
        if not isinstance(scalar, (int, float)):
            reads.append(scalar)
        return self.add(eng, lambda e: e.scalar_tensor_tensor(out, in0, scalar, in1, op0, op1),
                        reads=reads, writes=[out])

    def copy(self, out, in_, eng="dve"):
        if eng == "act":
            return self.add("act", lambda e: e.copy(out, in_), reads=[in_], writes=[out])
        return self.add(eng, lambda e: e.tensor_copy(out, in_), reads=[in_], writes=[out])

    def memset(self, ap, val, eng="pool"):
        return self.add(eng, lambda e: e.memset(ap, val), reads=[], writes=[ap])


import contextlib
from concourse.bass_utils import run_bass_kernel_spmd

D = 1024; S = 2048; NH = 8; DH = 64; DFF = 2816; INW = 3864
TB = 512; NT = TB // 128; NBLK = S // TB
BIG = 32768.0
EPS = 1e-6
TWO_PI = float(2 * np.pi)
PI = float(np.pi)
MAGIC = 12582912.0
C_U, C_Q, C_KV, C_G, C_MG = 0, 512, 1024, 1792, 1816
SCALE = DH ** -0.5


class _Stop(Exception):
    pass


def build(nseq, dbg=False, stop_after=99):
    nc = bass.Bass("TRN2", target_bir_lowering=False)
    P = Prog(nc)
    st = contextlib.ExitStack()

    def din(name, shape, dt=F32):
        P.notrack.add(name)
        return nc.dram_tensor(name, list(shape), dt, kind="ExternalInput").ap()

    def sb(name, shape, dt=F32):
        return st.enter_context(nc.sbuf_tensor(name, list(shape), dt))

    x = din("x", [nseq, S, D]); pos = din("pos", [nseq, S], I32)
    mix_g = din("mix_norm_g", [D]); w_in = din("w_in", [D, INW]); qg = din("q_norm_g", [DH])
    kg = din("k_norm_g", [3, DH]); cpe = din("cmp_pe", [2, 32, DH]); cw1 = din("cmp_w1", [2, 32, DH, 128])
    cw2 = din("cmp_w2", [2, 128, DH]); poolw = din("pool_w", [4, 128, 128]); pscale = din("pool_scale", [512])
    wpo = din("w_pool_out", [512, D]); wno = din("w_nsa_out", [512, D]); wout = din("w_out", [D, D])
    ffn_g = din("ffn_norm_g", [D]); wup = din("w_up", [D, 2 * DFF]); convw = din("conv_w", [3, DFF])
    convb = din("conv_b", [DFF]); wdn = din("w_down", [DFF, D])
    c_ident = din("c_ident", [128, 128]); c_mask = din("c_mask", [128, 2, 512]); c_cb8 = din("c_cb8", [8, 512])
    c_J = din("c_J", [8, 256]); c_EB = din("c_EB", [32, S]); c_ov = din("c_ov", [32, NBLK * 33])
    c_F = din("c_F", [128, 16, 32]); c_rope = din("c_rope", [64, 2]); c_perm = din("c_perm", [64, 64])
    c_invc = din("c_invc", [128, 4, 16])
    P.notrack.add("out")
    out = nc.dram_tensor("out", [nseq, S, D], F32, kind="ExternalOutput").ap()
    dbg_list = []

    def dump(name, ap, shape, dt=F32):
        if not dbg:
            return
        P.notrack.add(name)
        d = nc.dram_tensor(name, list(shape), dt, kind="ExternalOutput").ap()
        P.dma(d, ap, key="dbg_" + name, is_out=True)
        dbg_list.append(name)

    def stage(k):
        if k > stop_after:
            raise _Stop()

    with st:
        banks = [st.enter_context(nc.psum_tensor("pb%d" % i, [128, 512], F32)) for i in range(8)]
        rr = [0]

        def bank():
            i = rr[0] % 4
            rr[0] += 1
            return banks[i]

        ident = sb("ident", [128, 128], BF16); identf = sb("identf", [128, 128])
        maskb = sb("maskb", [128, 2, 512], BF16); cb8 = sb("cb8", [8, 512], BF16); Jm = sb("Jm", [8, 256], BF16)
        EB = sb("EB", [32, S], BF16); ovT = sb("ovT", [32, NBLK * 33], BF16); Fc = sb("Fc", [128, 16, 32])
        ropec = sb("ropec", [64, 2]); perm = sb("perm", [64, 64], BF16); ones64 = sb("ones64", [64, 64], BF16)
        invc = sb("invc", [128, 4, 16]); epsb = sb("epsb", [128, 1])
        P.dma(ident[:], c_ident, key="c0", eng="pool"); P.dma(identf[:], c_ident, key="c1")
        P.dma(maskb[:], c_mask, key="c2", eng="pool"); P.dma(cb8[:], c_cb8, key="c3", eng="pool")
        P.dma(Jm[:], c_J, key="c4", eng="pool"); P.dma(EB[:], c_EB, key="c5", eng="pool")
        P.dma(ovT[:], c_ov, key="c6", eng="pool"); P.dma(Fc[:], c_F, key="c7")
        P.dma(ropec[:], c_rope, key="c8"); P.dma(perm[:], c_perm, key="c9", eng="pool")
        P.dma(invc[:], c_invc, key="c10")
        P.memset(ones64[:], 1.0); P.memset(epsb[:], EPS)
        rows = sb("rows", [32, 9, 128]); colsb = sb("colsb", [128, 9, 32])
        P.memset(rows[:], 0.0)
        P.dma(rows[0:8, 0, :], mix_g.rearrange("(a p) -> a p", p=128), key="r0")
        P.dma(rows[0:8, 1, :], ffn_g.rearrange("(a p) -> a p", p=128), key="r1")
        P.dma(rows[0:4, 2, :], pscale.rearrange("(a p) -> a p", p=128), key="r2")
        P.dma(rows[0:22, 3, :], convb.rearrange("(a p) -> a p", p=128), key="r3")
        for k in range(3):
            P.dma(rows[0:22, 4 + k, :], convw[k].rearrange("(a p) -> a p", p=128), key="r4")
        for kv in range(2):
            P.dma(rows[0:32, 7 + kv, 0:64], cpe[kv], key="r5")
        for i in range(9):
            pb = bank()
            P.transpose(pb[:, 0:32], rows[:, i, :], identf[0:32, 0:32])
            P.copy(colsb[:, i, :], pb[:, 0:32])
        gmix = colsb[:, 0, :]; gffn = colsb[:, 1, :]; pscol = colsb[:, 2, :]; convb_sb = colsb[:, 3, :]
        pes = sb("pes", [64, 2, 32], BF16)
        for kv in range(2):
            P.copy(pes[:, kv, :], colsb[0:64, 7 + kv, :])
        qgc = sb("qgc", [64, 1]); kgc = sb("kgc", [64, 3])
        P.dma(qgc[:], qg.rearrange("(d o) -> d o", o=1), key="r6")
        for k in range(3):
            P.dma(kgc[:, k:k + 1], kg[k].rearrange("(d o) -> d o", o=1), key="r7")
        w2k = sb("w2k", [128, 64], BF16); w2v = sb("w2v", [128, 64], BF16); poolw_sb = sb("poolw_sb", [128, 4, 128], BF16)
        P.dma(w2k[:], cw2[0], key="w2", eng="pool"); P.dma(w2v[:], cw2[1], key="w2", eng="pool")
        P.dma(poolw_sb[:], poolw.rearrange("g i o -> i g o"), key="pw", eng="pool")
        hb = sb("hb", [128, 2])

        xres = sb("xres", [128, NT, D]); hT = sb("hT", [128, 8, TB], BF16); mT = sb("mT", [128, 8, TB], BF16)
        qT = sb("qT", [64, NT, 8, 128], BF16); pp = sb("pp", [128, 4, TB], BF16); OT = sb("OT", [128, 4, TB], BF16)
        ksT = sb("ksT", [64, 2, S], BF16); kwT = sb("kwT", [64, 2, S], BF16)
        vs1 = sb("vs1", [128, 16, 2, 65], BF16); vw1 = sb("vw1", [128, 16, 2, 65], BF16)
        K2 = sb("K2", [64, 4, TB + 16], BF16); G16 = sb("G16", [64, 16, TB // 16 + 1], BF16)
        kcT = sb("kcT", [64, NBLK, 2, 32], BF16); vc1 = sb("vc1", [32, NBLK, 2, 65], BF16)
        Ctab = sb("Ctab", [64, TB]); Stab = sb("Stab", [64, TB])
        gtm = sb("gtm", [128, NT, 24]); utail = sb("utail", [128, 4, 16]); halo = sb("halo", [128, 22, 2])
        slots = [sb("slot%d" % i, [128, 8, 512], BF16) for i in range(3)]
        srr = [0]

        def slot():
            i = srr[0] % 3
            srr[0] += 1
            return slots[i], "ws%d" % i

        wdsb = sb("wdsb", [128, 4, D], BF16); actT = sb("actT", [128, 4, TB], BF16)
        hn = [sb("hn%d" % i, [128, D], BF16) for i in range(1)]; junk = sb("junk", [128, D], BF16)
        ssq = sb("ssq", [128, NT]); rstd = sb("rstd", [128, NT])
        ubuf = sb("ubuf", [128, TB + 16]); pa = sb("pa", [128, TB + 16]); pbf = sb("pbf", [128, TB + 16])
        pooled = sb("pooled", [128, TB], BF16)
        t_sq = sb("t_sq", [64, 512], BF16); t_sd = sb("t_sd", [64, 512]); t_rs = sb("t_rs", [64, 512])
        t_qn = sb("t_qn", [64, 512]); t_qnb = sb("t_qnb", [64, 512], BF16); t_1 = sb("t_1", [64, 512]); t_2 = sb("t_2", [64, 512])
        posi = sb("posi", [64, TB], I32); ang = sb("ang", [64, TB]); ra = sb("ra", [64, TB]); rb = sb("rb", [64, TB])
        Eb = [sb("Eb%d" % i, [128, 512], BF16) for i in range(3)]
        Ec = [sb("Ec%d" % i, [32, 512], BF16) for i in range(NBLK)]
        hid = sb("hid", [128, 32], BF16)
        den = sb("den", [128, 3, 4]); fcm = sb("fcm", [128, 3, 4]); oacc = sb("oacc", [128, 4, 64]); otmp = sb("otmp", [128, 4, 64])
        Otm = sb("Otm", [128, 512], BF16)
        impt = sb("impt", [128, 4, 32]); imp = sb("imp", [128, 32]); imp2 = sb("imp2", [128, 32]); m8 = sb("m8", [128, 16])
        negb = sb("negb", [128, 32], BF16); negbT = sb("negbT", [32, 512], BF16); rci = sb("rci", [128, 4])
        gsg = [sb("gsg%d" % i, [128, TB]) for i in range(2)]; tmpf = sb("tmpf", [128, TB])
        gbuf = sb("gbuf", [128, TB + 2]); abuf = sb("abuf", [128, TB]); sact = sb("sact", [128, TB])
        if dbg:
            print("sbuf bytes remaining", nc.sbuf_bytes_remaining)
        P.memset(vs1[:], 1.0); P.memset(vw1[:], 1.0); P.memset(vc1[:], 1.0)
        erot = [0]

        w1buf = sb("w1buf", [64, 32, 128], BF16)

        for kv in range(2):
            w1s_ = w1buf[:]
            P.dma(w1s_, cw1[kv].rearrange("l d h -> d l h"), key="w1b", eng="pool")
            pb = bank()
            for l in range(32):
                P.mm(pb[:, 0:1], w1s_[:, l, :], pes[:, kv, l:l + 1], start=(l == 0), stop=(l == 31))
            P.copy(hb[:, kv:kv + 1], pb[:, 0:1])

        def wload(dst, src, key):
            P.dma(dst, src, key=key, eng="pool")

        def w_cols(wmat, c0, ncol):
            return wmat.rearrange("(kc p) c -> p kc c", p=128)[:, :, c0:c0 + ncol]

        def proj_fm(ps, wsl, c0, M):
            for kc in range(8):
                P.mm(ps[0:M, 0:TB], wsl[:, kc, c0:c0 + M], hT[:, kc, :], start=(kc == 0), stop=(kc == 7))

        def norm_T(gcol):
            for tt in range(NT):
                P.act(junk[:], xres[:, tt, :], AF.Square, accum_out=ssq[:, tt:tt + 1])
            P.act(rstd[:], ssq[:], AF.Sqrt, bias=epsb[:, 0:1], scale=1.0 / D)
            P.add("dve", lambda e: e.reciprocal(rstd[:], rstd[:]), reads=[rstd[:]], writes=[rstd[:]])
            for tt in range(NT):
                h_ = hn[0]
                P.ts(h_[:], xres[:, tt, :], rstd[:, tt:tt + 1], ALU.mult)
                pb = bank()
                pbb = pb[:].bitcast(BF16)
                for kc in range(8):
                    P.transpose(pbb[:, kc * 128:(kc + 1) * 128], h_[:, kc * 128:(kc + 1) * 128], ident[:])
                for kc in range(8):
                    P.ts(hT[:, kc, tt * 128:(tt + 1) * 128], pbb[:, kc * 128:(kc + 1) * 128], gcol[:, kc:kc + 1], ALU.mult)

        def normrope(ps, gvec, cs, ss, out_ap, n, out3=None):
            P.act(t_sq[:, 0:n], ps, AF.Square)
            p2 = bank()
            P.mm(p2[0:64, 0:n], ones64[:], t_sq[:, 0:n])
            P.act(t_sd[:, 0:n], p2[0:64, 0:n], AF.Sqrt, bias=epsb[0:64, 0:1], scale=1.0 / DH)
            P.add("dve", lambda e: e.reciprocal(t_rs[:, 0:n], t_sd[:, 0:n]), reads=[t_sd[:, 0:n]], writes=[t_rs[:, 0:n]])
            P.stt(t_qn[:, 0:n], ps, gvec, t_rs[:, 0:n], ALU.mult, ALU.mult)
            P.copy(t_qnb[:, 0:n], t_qn[:, 0:n], eng="pool")
            p3 = bank()
            P.mm(p3[0:64, 0:n], perm[:], t_qnb[:, 0:n])
            P.tt(t_1[:, 0:n], t_qn[:, 0:n], cs, ALU.mult)
            P.tt(t_2[:, 0:n], p3[0:64, 0:n], ss, ALU.mult)
            if out3 is not None:
                P.tt(out3, t_1[:, 0:n].rearrange("p (a b) -> p a b", b=128), t_2[:, 0:n].rearrange("p (a b) -> p a b", b=128), ALU.add)
            else:
                P.tt(out_ap, t_1[:, 0:n], t_2[:, 0:n], ALU.add)

        def sincos(dst, shift, sign_col):
            P.ts(ra[:], ang[:], 1.0 / TWO_PI, ALU.mult, shift / TWO_PI, ALU.add)
            P.ts(ra[:], ra[:], MAGIC, ALU.add, MAGIC, ALU.subtract)
            P.stt(rb[:], ra[:], -TWO_PI, ang[:], ALU.mult, ALU.add)
            if shift != 0.0:
                P.ts(rb[:], rb[:], shift, ALU.add)
            P.ts(rb[:], rb[:], PI, ALU.min, -PI, ALU.max)
            P.act(ra[:], rb[:], AF.Sin)
            if sign_col is None:
                P.copy(dst, ra[:])
            else:
                P.ts(dst, ra[:], sign_col, ALU.mult)

        def exp_tile(ps, nk):
            e_ = Eb[erot[0] % 3]
            erot[0] += 1
            P.act(e_[0:nk, :], ps[0:nk, :], AF.Exp, scale=SCALE)
            return e_

        try:
          for s_i in range(nseq):
              P.memset(utail[:], 0.0); P.memset(halo[:], 0.0); P.memset(K2[:], 0.0)
              for blk in range(NBLK):
                  T0 = blk * TB
                  QT0 = blk * NT
                  for tt in range(NT):
                      P.dma(xres[:, tt, :], x[s_i, T0 + tt * 128:T0 + (tt + 1) * 128, :], key="x")
                  stage(1)
                  norm_T(gmix)
                  stage(2)
                  P.dma(posi[:], pos[s_i:s_i + 1, T0:T0 + TB].partition_broadcast(64), key="pos")
                  P.copy(rb[:], posi[:])
                  P.ts(ang[:], rb[:], ropec[:, 0:1], ALU.mult)
                  sincos(Stab[:], 0.0, ropec[:, 1:2])
                  sincos(Ctab[:], PI / 2, None)
                  stage(3)
                  sl, sk = slot(); wload(sl[:, :, 0:512], w_cols(w_in, C_U, 512), sk)
                  for gi, wdw in enumerate((2, 4, 8, 16)):
                      pb = bank()
                      proj_fm(pb, sl, gi * 128, 128)
                      P.copy(ubuf[:, 0:16], utail[:, gi, :])
                      P.copy(ubuf[:, 16:16 + TB], pb[:, 0:TB], eng="act")
                      P.copy(utail[:, gi, :], ubuf[:, TB:TB + 16])
                      src = ubuf; sh = 1; lo = 1; tog = 0
                      while sh < wdw:
                          dst = pa if tog == 0 else pbf
                          P.tt(dst[:, lo:TB + 16], src[:, lo:TB + 16], src[:, lo - sh:TB + 16 - sh], ALU.add)
                          src = dst; tog ^= 1; sh *= 2; lo = 2 * sh - 1
                      P.stt(pooled[:], src[:, 16:16 + TB], 1.0 / wdw, ubuf[:, 16:16 + TB], ALU.mult, ALU.subtract)
                      if blk == 0:
                          P.tt(tmpf[:, 0:16], src[:, 16:32], invc[:, gi, :], ALU.mult)
                          P.tt(pooled[:, 0:16], tmpf[:, 0:16], ubuf[:, 16:32], ALU.subtract)
                      pb2 = bank()
                      P.mm(pb2[:, 0:TB], poolw_sb[:, gi, :], pooled[:])
                      P.ts(pp[:, gi, :], pb2[:, 0:TB], pscol[:, gi:gi + 1], ALU.mult)
                  stage(4)
                  sl, sk = slot(); slv = sl[:].rearrange("p a b -> p (a b)").rearrange("p (g c) -> p g c", g=4)
                  wload(slv, wpo.rearrange("(g p) c -> p g c", p=128), sk)
                  for mc in range(8):
                      pb = bank()
                      for gi in range(4):
                          P.mm(pb[:, 0:TB], slv[:, gi, mc * 128:(mc + 1) * 128], pp[:, gi, :], start=(gi == 0), stop=(gi == 3))
                      P.copy(mT[:, mc, :], pb[:, 0:TB], eng="act")
                  stage(5)
                  sl, sk = slot(); wload(sl[:, :, 0:512], w_cols(w_in, C_Q, 512), sk)
                  for h in range(8):
                      pb = bank()
                      proj_fm(pb, sl, h * 64, 64)
                      normrope(pb[0:64, 0:TB], qgc[:, 0:1], Ctab[:], Stab[:], None, TB, out3=qT[:, :, h, :])
                  stage(6)
                  sl, sk = slot(); wload(sl[:, :, 0:512], w_cols(w_in, C_KV, 512), sk)
                  sl2, sk2 = slot(); wload(sl2[:, :, 0:256], w_cols(w_in, C_KV + 512, 256), sk2)
                  wload(sl2[:, :, 256:280], w_cols(w_in, C_G, 24), sk2)
                  for g in range(2):
                      pb = bank(); proj_fm(pb, sl, 256 + g * 64, 64)
                      normrope(pb[0:64, 0:TB], kgc[:, 1:2], Ctab[:], Stab[:], ksT[:, g, T0:T0 + TB], TB)
                      pb = bank(); proj_fm(pb, sl2, g * 64, 64)
                      normrope(pb[0:64, 0:TB], kgc[:, 2:3], Ctab[:], Stab[:], kwT[:, g, T0:T0 + TB], TB)
                  stage(7)
                  NB = TB // 16 - (1 if blk == 0 else 0)
                  base = 16 if blk == 0 else 0
                  ctab0 = 31 if blk == 0 else 15
                  for kv in range(2):
                      w1s_ = w1buf[:]
                      wload(w1s_, cw1[kv].rearrange("l d h -> d l h"), "w1b")
                      for g in range(2):
                          ki = kv * 2 + g
                          pb = bank(); proj_fm(pb, sl, kv * 128 + g * 64, 64)
                          if blk > 0:
                              P.copy(K2[:, ki, 0:16], K2[:, ki, TB:TB + 16])
                          P.copy(K2[:, ki, 16:16 + TB], pb[0:64, 0:TB], eng="act")
                          P.copy(G16[:, :, 0:NB + 1], K2[:, ki, base:base + 16 * (NB + 1)].rearrange("p (i j) -> p j i", j=16))
                          ph = bank()
                          for l in range(32):
                              P.mm(ph[:, 0:NB], w1s_[:, l, :], G16[:, l % 16, l // 16:l // 16 + NB], start=(l == 0), stop=(l == 31))
                          P.act(hid[:, 0:NB], ph[:, 0:NB], AF.Silu, bias=hb[:, kv:kv + 1])
                          p4 = bank()
                          if kv == 0:
                              P.mm(p4[0:64, 0:NB], w2k[:], hid[:, 0:NB])
                              cend = ctab0 + 16 * (NB - 1) + 1
                              normrope(p4[0:64, 0:NB], kgc[:, 0:1], Ctab[:, ctab0:cend:16], Stab[:, ctab0:cend:16], kcT[:, blk, g, 0:NB], NB)
                          else:
                              P.mm(p4[0:NB, 0:64], hid[:, 0:NB], w2v[:])
                              P.copy(vc1[0:NB, blk, g, 0:64], p4[0:NB, 0:64], eng="act")
                  stage(8)
                  for tt in range(NT):
                      pb = bank(); pc = bank()
                      for kc in range(8):
                          P.mm(pb[:, 0:128], hT[:, kc, tt * 128:(tt + 1) * 128], sl[:, kc, 384:512], start=(kc == 0), stop=(kc == 7))
                      for kc in range(8):
                          P.mm(pc[:, 0:152], hT[:, kc, tt * 128:(tt + 1) * 128], sl2[:, kc, 128:280], start=(kc == 0), stop=(kc == 7))
                      P.copy(vs1[:, QT0 + tt, :, 0:64], pb[:, 0:128].rearrange("p (g d) -> p g d", g=2), eng="act")
                      P.copy(vw1[:, QT0 + tt, :, 0:64], pc[:, 0:128].rearrange("p (g d) -> p g d", g=2), eng="act")
                      P.act(gtm[:, tt, :], pc[:, 128:152], AF.Sigmoid)
                  if dbg and blk == 0:
                      dump("d_qT", qT[:], [64, NT, 8, 128], BF16); dump("d_ksT", ksT[:, :, 0:TB], [64, 2, TB], BF16)
                      dump("d_kwT", kwT[:, :, 0:TB], [64, 2, TB], BF16)
                      dump("d_pp", pp[:], [128, 4, TB], BF16); dump("d_hT", hT[:], [128, 8, TB], BF16)
                      dump("d_kcT", kcT[:, 0, :, 0:31], [64, 2, 31], BF16); dump("d_vc1", vc1[0:31, 0, :, :], [31, 2, 65], BF16)
                      dump("d_gtm", gtm[:], [128, NT, 24]); dump("d_vs1", vs1[:, 0:NT, :, :], [128, NT, 2, 65], BF16)
                      dump("d_Ctab", Ctab[:], [64, TB]); dump("d_Stab", Stab[:], [64, TB]); dump("d_mT0", mT[:], [128, 8, TB], BF16)

                  stage(9)
                  for ql in range(NT):
                      qt = QT0 + ql
                      for g in range(2):
                          rhsQ = qT[:, ql, 4 * g:4 * g + 4, :].rearrange("p h q -> p (h q)")
                          Oc, Os, Ow = banks[4], banks[5], banks[6]
                          tiles = []
                          for b in range(blk + 1):
                              n0 = 0 if b == 0 else b * (TB // 16) - 1
                              nbb = TB // 16 - (1 if b == 0 else 0)
                              nk = min(nbb, 8 * qt + 7 - n0)
                              if nk <= 0:
                                  continue
                              tiles.append((b, n0, nk, b == blk))
                          for ti, (b, n0, nk, masked) in enumerate(tiles):
                              ps = bank()
                              P.mm(ps[0:nk, :], kcT[:, b, g, 0:nk], rhsQ, start=True, stop=not masked)
                              if masked:
                                  s0 = 121 - 8 * qt + n0
                                  P.mm(ps[0:nk, :], Jm[:, s0:s0 + nk], cb8[:], start=False, stop=True)
                              P.act(Ec[b][0:nk, :], ps[0:nk, :], AF.Exp, scale=SCALE)
                          for r in range(4):
                              for ti, (b, n0, nk, masked) in enumerate(tiles):
                                  P.mm(Oc[:, r * 128:r * 128 + 65], Ec[b][0:nk, r * 128:(r + 1) * 128], vc1[0:nk, b, g, :],
                                       start=(ti == 0 and r == 0), stop=(ti == len(tiles) - 1 and r == 3))
                          use_sel = qt >= 8
                          if use_sel:
                              pi_ = bank()
                              for r in range(4):
                                  for ti, (b, n0, nk, masked) in enumerate(tiles):
                                      P.mm(pi_[:, r * 64:r * 64 + 33], Ec[b][0:nk, r * 128:(r + 1) * 128], ovT[0:nk, b * 33:(b + 1) * 33],
                                           start=(ti == 0 and r == 0), stop=(ti == len(tiles) - 1 and r == 3))
                              piv = pi_[:, 0:256].rearrange("p (r c) -> p r c", c=64)
                              P.add("dve", lambda e, a=piv: e.reciprocal(rci[:], a[:, :, 32]), reads=[pi_[:, 0:256]], writes=[rci[:]])
                              P.tt(impt[:], piv[:, :, 0:32], rci[:].unsqueeze(2).to_broadcast([128, 4, 32]), ALU.mult)
                              P.add("dve", lambda e: e.tensor_reduce(imp[:], impt[:].rearrange("p r j -> p j r"), AX.X, ALU.add),
                                    reads=[impt[:]], writes=[imp[:]])
                              P.tt(imp2[:], imp[:], Fc[:, qt, :], ALU.max)
                              P.add("dve", lambda e: e.max(out=m8[:, 0:8], in_=imp2[:]), reads=[imp2[:]], writes=[m8[:, 0:8]])
                              P.add("dve", lambda e: e.match_replace(out=imp[:], in_to_replace=m8[:, 0:8], in_values=imp2[:], imm_value=-1e30),
                                    reads=[imp2[:], m8[:, 0:8]], writes=[imp[:]])
                              P.add("dve", lambda e: e.max(out=m8[:, 8:16], in_=imp[:]), reads=[imp[:]], writes=[m8[:, 8:16]])
                              P.ts(negb[:], imp2[:], m8[:, 15:16], ALU.is_lt, -BIG, ALU.mult)
                              pt = bank(); ptb = pt[:].bitcast(BF16)
                              for r in range(4):
                                  P.transpose(ptb[0:32, r * 128:(r + 1) * 128], negb[:], ident[:])
                              P.copy(negbT[:], ptb[0:32, 0:512])
                          for kt in range(qt + 1):
                              ps = bank()
                              last_plain = not (use_sel or kt == qt)
                              P.mm(ps[:, :], ksT[:, g, kt * 128:(kt + 1) * 128], rhsQ, start=True, stop=last_plain)
                              if use_sel:
                                  P.mm(ps[:, :], EB[:, kt * 128:(kt + 1) * 128], negbT[:], start=False, stop=(kt != qt))
                              if kt == qt:
                                  P.mm(ps[:, :], ident[:], maskb[:, 0, :], start=False, stop=True)
                              e_ = exp_tile(ps, 128)
                              for r in range(4):
                                  P.mm(Os[:, r * 128:r * 128 + 65], e_[:, r * 128:(r + 1) * 128], vs1[:, kt, g, :], start=(kt == 0 and r == 0), stop=(kt == qt and r == 3))
                          k0 = max(0, qt - 4)
                          for kt in range(k0, qt + 1):
                              ps = bank()
                              diag = kt == qt
                              bandt = kt == qt - 4
                              P.mm(ps[:, :], kwT[:, g, kt * 128:(kt + 1) * 128], rhsQ, start=True, stop=not (diag or bandt))
                              if diag:
                                  P.mm(ps[:, :], ident[:], maskb[:, 0, :], start=False, stop=True)
                              if bandt:
                                  P.mm(ps[:, :], ident[:], maskb[:, 1, :], start=False, stop=True)
                              e_ = exp_tile(ps, 128)
                              for r in range(4):
                                  P.mm(Ow[:, r * 128:r * 128 + 65], e_[:, r * 128:(r + 1) * 128], vw1[:, kt, g, :], start=(kt == k0 and r == 0), stop=(kt == qt and r == 3))
                          for bi, Ob in enumerate((Oc, Os, Ow)):
                              ov_ = Ob[:, :].rearrange("p (r c) -> p r c", c=128)
                              P.ts(den[:, bi, :], ov_[:, :, 64], 1e-30, ALU.max)
                          P.add("dve", lambda e: e.reciprocal(den[:], den[:]), reads=[den[:]], writes=[den[:]])
                          gv = gtm[:, ql, 12 * g:12 * g + 12].rearrange("p (h b) -> p b h", b=3)
                          P.tt(fcm[:], den[:], gv, ALU.mult)
                          for bi, Ob in enumerate((Oc, Os, Ow)):
                              ov_ = Ob[:, :].rearrange("p (r c) -> p r c", c=128)
                              fb = fcm[:, bi, :].unsqueeze(2).to_broadcast([128, 4, 64])
                              if bi == 0:
                                  P.tt(oacc[:], ov_[:, :, 0:64], fb, ALU.mult)
                              else:
                                  P.tt(otmp[:], ov_[:, :, 0:64], fb, ALU.mult)
                                  dst = oacc[:] if bi == 1 else Otm[:, g * 256:(g + 1) * 256].rearrange("p (r d) -> p r d", d=64)
                                  P.tt(dst, oacc[:], otmp[:], ALU.add)
                      pt = bank(); ptb = pt[:].bitcast(BF16)
                      for c4 in range(4):
                          P.transpose(ptb[:, c4 * 128:(c4 + 1) * 128], Otm[:, c4 * 128:(c4 + 1) * 128], ident[:])
                      P.copy(OT[:, :, ql * 128:(ql + 1) * 128], ptb[:, 0:512].rearrange("p (c q) -> p c q", c=4), eng="act")
                  if dbg and blk == 0:
                      dump("d_OT", OT[:], [128, 4, TB], BF16)

                  stage(10)
                  for mb in range(2):
                      sln, skn = slot(); slnv = sln[:].rearrange("p a b -> p (a b)").rearrange("p (g c) -> p g c", g=4)
                      wload(slnv, wno.rearrange("(g p) c -> p g c", p=128), skn)
                      sgp, skp = slot(); wload(sgp[:, :, 0:512], w_cols(w_in, C_MG + mb * 512, 512), skp)
                      sgn, skg = slot(); wload(sgn[:, :, 0:512], w_cols(w_in, C_MG + 1024 + mb * 512, 512), skg)
                      for mi in range(4):
                          mc = mb * 4 + mi
                          p_gp = bank(); proj_fm(p_gp, sgp, mi * 128, 128)
                          p_gn = bank(); proj_fm(p_gn, sgn, mi * 128, 128)
                          p_yn = bank()
                          for c4 in range(4):
                              P.mm(p_yn[:, 0:TB], slnv[:, c4, mc * 128:(mc + 1) * 128], OT[:, c4, :], start=(c4 == 0), stop=(c4 == 3))
                          P.act(gsg[0][:], p_gp[:, 0:TB], AF.Sigmoid)
                          P.act(gsg[1][:], p_gn[:, 0:TB], AF.Sigmoid)
                          P.tt(tmpf[:], p_yn[:, 0:TB], gsg[1][:], ALU.mult)
                          P.tt(gsg[0][:], gsg[0][:], mT[:, mc, :], ALU.mult)
                          P.tt(mT[:, mc, :], gsg[0][:], tmpf[:], ALU.add)
                  sa, ska = slot(); sbb, skb = slot()
                  wo_v = wout.rearrange("(kc p) c -> p kc c", p=128)
                  wload(sa[:, :, :], wo_v[:, :, 0:512], ska); wload(sbb[:, :, :], wo_v[:, :, 512:1024], skb)
                  for tt in range(NT):
                      for hh, sw in enumerate((sa, sbb)):
                          pb = bank()
                          for kc in range(8):
                              P.mm(pb[:, :], mT[:, kc, tt * 128:(tt + 1) * 128], sw[:, kc, :], start=(kc == 0), stop=(kc == 7))
                          P.tt(xres[:, tt, hh * 512:(hh + 1) * 512], xres[:, tt, hh * 512:(hh + 1) * 512], pb[:, :], ALU.add)
                  if dbg and blk == 0:
                      dump("d_x1", xres[:], [128, NT, D])

                  stage(11)
                  norm_T(gffn)
                  for f0 in range(0, 22, 4):
                      nf = min(4, 22 - f0)
                      wload(wdsb[:, 0:nf, :], wdn[f0 * 128:(f0 + nf) * 128, :].rearrange("(fc p) c -> p fc c", p=128), "wd")
                      sg_, skg_ = slot(); wload(sg_[:, :, 0:nf * 128], w_cols(wup, f0 * 128, nf * 128), skg_)
                      sv_, skv_ = slot(); wload(sv_[:, :, 0:nf * 128], w_cols(wup, DFF + f0 * 128, nf * 128), skv_)
                      for fi in range(nf):
                          f = f0 + fi
                          p_g = bank(); proj_fm(p_g, sg_, fi * 128, 128)
                          p_v = bank(); proj_fm(p_v, sv_, fi * 128, 128)
                          P.copy(gbuf[:, 0:2], halo[:, f, :])
                          P.copy(gbuf[:, 2:2 + TB], p_g[:, 0:TB], eng="act")
                          P.copy(halo[:, f, :], gbuf[:, TB:TB + 2])
                          P.ts(abuf[:], gbuf[:, 2:2 + TB], colsb[:, 6, f:f + 1], ALU.mult, convb_sb[:, f:f + 1], ALU.add)
                          P.stt(abuf[:], gbuf[:, 1:1 + TB], colsb[:, 5, f:f + 1], abuf[:], ALU.mult, ALU.add)
                          P.stt(abuf[:], gbuf[:, 0:TB], colsb[:, 4, f:f + 1], abuf[:], ALU.mult, ALU.add)
                          P.act(sact[:], abuf[:], AF.Silu)
                          P.tt(actT[:, fi, :], sact[:], p_v[:, 0:TB], ALU.mult)
                      for tt in range(NT):
                          for hh in range(2):
                              pb = bank()
                              for fi in range(nf):
                                  P.mm(pb[:, :], actT[:, fi, tt * 128:(tt + 1) * 128], wdsb[:, fi, hh * 512:(hh + 1) * 512], start=(fi == 0), stop=(fi == nf - 1))
                              P.tt(xres[:, tt, hh * 512:(hh + 1) * 512], xres[:, tt, hh * 512:(hh + 1) * 512], pb[:, :], ALU.add)
                  for tt in range(NT):
                      P.dma(out[s_i, T0 + tt * 128:T0 + (tt + 1) * 128, :], xres[:, tt, :], key="o", is_out=True)
        except _Stop:
            for tt in range(NT):
                P.dma(out[0, tt * 128:(tt + 1) * 128, :], xres[:, tt, :], key="o", is_out=True)
        P.emit()
    return nc, P, dbg_list


def host_consts():
    c = {}
    c["c_ident"] = np.eye(128, dtype=np.float32)
    k = np.arange(128)[:, None]; q = np.arange(128)[None, :]
    caus = np.where(k <= q, 0.0, -BIG).astype(np.float32)
    band = np.where(k > q, 0.0, -BIG).astype(np.float32)
    c["c_mask"] = np.stack([np.tile(caus, (1, 4)), np.tile(band, (1, 4))], axis=1).astype(np.float32)
    i8 = np.arange(8)[:, None]
    cb = np.where(16 * i8 + 15 <= q, 0.0, -BIG).astype(np.float32)
    c["c_cb8"] = np.tile(cb, (1, 4)).astype(np.float32)
    J = np.zeros((8, 256), np.float32)
    for i in range(8):
        J[i, i + 120] = 1.0
    c["c_J"] = J
    EBm = np.zeros((32, S), np.float32)
    for j in range(32):
        EBm[j, j * 64:(j + 1) * 64] = 1.0
    c["c_EB"] = EBm
    n_cmp = 127
    sel_start = np.arange(32)[:, None] * 64
    cmp_start = np.arange(n_cmp)[None, :] * 16
    ov = np.clip(np.minimum(sel_start + 64, cmp_start + 32) - np.maximum(sel_start, cmp_start), 0, None) / 32.0
    ovfull = np.zeros((n_cmp, 33), np.float32); ovfull[:, 0:32] = ov.T; ovfull[:, 32] = 1.0
    ovt = np.zeros((NBLK, 32, 33), np.float32)
    for b in range(NBLK):
        n0 = 0 if b == 0 else b * (TB // 16) - 1
        nbb = TB // 16 - (1 if b == 0 else 0)
        ovt[b, 0:nbb] = ovfull[n0:n0 + nbb]
    c["c_ov"] = np.ascontiguousarray(ovt.transpose(1, 0, 2).reshape(32, NBLK * 33))
    F = np.zeros((16, 128, 32), np.float32)
    for qt in range(16):
        t = qt * 128 + np.arange(128)
        cur = t // 64
        for j in range(32):
            F[qt, :, j] = np.where((j == 0) | (j == cur) | (j == cur - 1), 1e9, 0.0)
    c["c_F"] = np.ascontiguousarray(F.transpose(1, 0, 2))
    rope = np.zeros((64, 2), np.float32)
    inv = (500000.0 ** (-(np.arange(8, dtype=np.float32) * 2.0 / 16))).astype(np.float32)
    for d in range(16):
        rope[d, 0] = inv[d % 8]
        rope[d, 1] = -1.0 if d < 8 else 1.0
    c["c_rope"] = rope
    pm = np.zeros((64, 64), np.float32)
    for m in range(16):
        pm[m + 8 if m < 8 else m - 8, m] = 1.0
    c["c_perm"] = pm
    invc = np.zeros((128, 4, 16), np.float32)
    for gi, w in enumerate((2, 4, 8, 16)):
        invc[:, gi, :] = 1.0 / np.minimum(np.arange(16) + 1.0, float(w))
    c["c_invc"] = invc
    return c


_CACHE = {}


def kernel(**inputs):
    n = 8
    B = inputs["x"].shape[0]
    nseq = B // n
    if "nc" not in _CACHE:
        _CACHE["nc"] = build(nseq)[0]
    nc = _CACHE["nc"]
    consts = host_consts()
    shared = {}
    for k_, v_ in inputs.items():
        if k_ in ("x", "positions"):
            continue
        a = np.ascontiguousarray(v_)
        shared[k_] = a[0] if a.shape[0] == 1 else a
    in_maps = []
    for c_ in range(n):
        m = dict(shared); m.update(consts)
        m["x"] = np.ascontiguousarray(inputs["x"][c_ * nseq:(c_ + 1) * nseq])
        m["pos"] = np.ascontiguousarray(inputs["positions"][c_ * nseq:(c_ + 1) * nseq]).astype(np.int32)
        in_maps.append(m)
    res = run_bass_kernel_spmd(nc, in_maps, core_ids=list(range(n)))
    return np.concatenate([r["out"] for r in res.results], axis=0).astype(np.float32)
```

```python
import numpy as np
import concourse.bass as bass
import concourse.mybir as mybir

F32 = mybir.dt.float32
BF16 = mybir.dt.bfloat16
I32 = mybir.dt.int32
AF = mybir.ActivationFunctionType
ALU = mybir.AluOpType
AX = mybir.AxisListType

_ESZ = {F32: 4, BF16: 2, I32: 4}


def esize(dt):
    if dt in _ESZ:
        return _ESZ[dt]
    s = str(dt)
    if '64' in s:
        return 8
    if '32' in s:
        return 4
    if '16' in s:
        return 2
    return 1


def bbox(ap):
    dims = ap.ap
    es = esize(ap.dtype)
    name = ap.tensor.name
    if name.startswith("pb"):
        return (name, 0, 128, 0, 2048)
    off = int(ap.offset)
    pstep, pcnt = dims[0]
    if pstep == 0:
        p0 = 0
        col = off
        pcnt_eff = 1
    else:
        p0 = off // pstep
        col = off % pstep
        pcnt_eff = pcnt
    ext = 0
    for st, cnt in dims[1:]:
        ext += abs(st) * (cnt - 1)
    return (name, p0, p0 + pcnt_eff, col * es, (col + ext + 1) * es)


class Op:
    __slots__ = ("eng", "fn", "deps", "dma_key", "idx", "marked", "mval", "waits", "dma_cnt")

    def __init__(self, eng, fn, dma_key):
        self.eng = eng
        self.fn = fn
        self.deps = set()
        self.dma_key = dma_key
        self.marked = False
        self.mval = 0
        self.waits = None
        self.dma_cnt = 0


ENGS = ("pe", "act", "dve", "pool", "sp")


class Prog:
    def __init__(self, nc):
        self.nc = nc
        self.ops = []
        self.live = {}
        self.dma_count = {}
        self.out_keys = set()
        self.notrack = set()

    def add(self, eng, fn, reads=(), writes=(), dma_key=None, is_out=False):
        op = Op(eng, fn, dma_key)
        op.idx = len(self.ops)
        self.ops.append(op)
        if dma_key is not None:
            c = self.dma_count.get(dma_key, 0) + 1
            self.dma_count[dma_key] = c
            op.dma_cnt = c
            if is_out:
                self.out_keys.add(dma_key)
        for ap in reads:
            self._access(op, ap, False)
        for ap in writes:
            self._access(op, ap, True)
        return op

    def _is_dma(self, op):
        return op.dma_key is not None

    def _access(self, op, ap, is_write):
        name, p0, p1, b0, b1 = bbox(ap)
        if name in self.notrack:
            return
        L = self.live.get(name)
        if L is None:
            L = []
            self.live[name] = L
        ops = self.ops
        newL = []
        for a in L:
            ap0, ap1, ab0, ab1, aw, aidx = a
            if aidx == op.idx:
                newL.append(a)
                continue
            if ap0 < p1 and p0 < ap1 and ab0 < b1 and b0 < ab1:
                prod = ops[aidx]
                if is_write or aw:
                    same = (prod.eng == op.eng) and not self._is_dma(prod) and not self._is_dma(op)
                    if same:
                        if op.eng != "pe":
                            op.deps.add(aidx)
                    else:
                        op.deps.add(aidx)
                contained = (p0 <= ap0 and ap1 <= p1 and b0 <= ab0 and ab1 <= b1)
                if contained:
                    if is_write:
                        continue
                    if (not aw) and prod.eng == op.eng and not self._is_dma(prod) and not self._is_dma(op):
                        continue
            newL.append(a)
        newL.append((p0, p1, b0, b1, is_write, op.idx))
        if len(newL) > 48:
            newL = self._compact(newL)
        self.live[name] = newL

    def _compact(self, L):
        ops = self.ops
        keep = []
        merged = {}
        for a in L:
            p0, p1, b0, b1, w, idx = a
            o = ops[idx]
            if w or self._is_dma(o):
                keep.append(a)
                continue
            m = merged.get(o.eng)
            if m is None:
                merged[o.eng] = [p0, p1, b0, b1, False, idx]
            else:
                m[0] = min(m[0], p0); m[1] = max(m[1], p1)
                m[2] = min(m[2], b0); m[3] = max(m[3], b1)
                m[5] = max(m[5], idx)
        for m in merged.values():
            keep.append(tuple(m))
        return keep

    def emit(self):
        nc = self.nc
        ops = self.ops
        for op in ops:
            best = {}
            keep = set()
            for d in op.deps:
                p = ops[d]
                if p.dma_key is not None:
                    keep.add(d)
                else:
                    if best.get(p.eng, -1) < d:
                        best[p.eng] = d
            keep.update(best.values())
            op.deps = keep
        for op in ops:
            for d in op.deps:
                ops[d].marked = True
        cnt = {e: 0 for e in ENGS}
        for op in ops:
            if op.dma_key is None and op.marked:
                cnt[op.eng] += 1
                op.mval = cnt[op.eng]
        seen = {e: {} for e in ENGS}
        dma_running = {}
        for op in ops:
            need = {}
            for d in op.deps:
                p = ops[d]
                if p.dma_key is not None:
                    key = ("dma", p.dma_key)
                    val = 16 * dma_running[p.dma_key]
                else:
                    key = ("eng", p.eng)
                    val = p.mval
                if need.get(key, 0) < val:
                    need[key] = val
            s = seen[op.eng]
            w = []
            for key, val in need.items():
                if s.get(key, 0) < val:
                    s[key] = val
                    w.append((key, val))
            op.waits = w
            if op.dma_key is not None:
                dma_running[op.dma_key] = op.dma_cnt
        import contextlib
        stack = contextlib.ExitStack()
        sems = {}
        for e in ENGS:
            sems[("eng", e)] = stack.enter_context(nc.semaphore("s_" + e))
        for k in self.dma_count:
            sems[("dma", k)] = stack.enter_context(nc.semaphore("d_" + str(k)))
        per_eng = {e: [op for op in ops if op.eng == e] for e in ENGS}
        finals = [(("dma", k), 16 * self.dma_count[k]) for k in self.out_keys]

        def run(engobj, elist, final=False):
            for op in elist:
                for key, val in op.waits:
                    engobj.wait_ge(sems[key], val)
                ins = op.fn(engobj)
                if op.dma_key is not None:
                    ins.then_inc(sems[("dma", op.dma_key)], 16)
                elif op.marked:
                    ins.then_inc(sems[("eng", op.eng)], 1)
            if final:
                for key, val in finals:
                    engobj.wait_ge(sems[key], val)

        with stack:
            with nc.Block() as block:
                @block.tensor
                def _(e):
                    run(e, per_eng["pe"])

                @block.scalar
                def _(e):
                    run(e, per_eng["act"])

                @block.vector
                def _(e):
                    run(e, per_eng["dve"])

                @block.gpsimd
                def _(e):
                    run(e, per_eng["pool"])

                @block.sync
                def _(e):
                    run(e, per_eng["sp"], final=True)
        self.stats = {e: len(per_eng[e]) for e in ENGS}
        self.stats["sem_max"] = dict(cnt)
        self.stats["nsem"] = len(sems)
        return nc

    def dma(self, out, in_, key, eng="sp", is_out=False, **kw):
        return self.add(eng, lambda e: e.dma_start(out=out, in_=in_, **kw),
                        reads=[in_], writes=[out], dma_key=key, is_out=is_out)

    def mm(self, out, lhsT, rhs, start=True, stop=True, **kw):
        return self.add("pe", lambda e: e.matmul(out, lhsT, rhs, start=start, stop=stop, **kw),
                        reads=[lhsT, rhs], writes=[out])

    def transpose(self, out, in_, ident):
        return self.add("pe", lambda e: e.transpose(out, in_, ident),
                        reads=[in_, ident], writes=[out])

    def act(self, out, in_, func, bias=None, scale=None, accum_out=None, eng="act"):
        reads = [in_]
        kw = {}
        if bias is not None:
            kw["bias"] = bias
            if not isinstance(bias, (int, float)):
                reads.append(bias)
        if scale is not None:
            kw["scale"] = scale
            if not isinstance(scale, (int, float)):
                reads.append(scale)
        writes = [out]
        if accum_out is not None:
            kw["accum_out"] = accum_out
            writes.append(accum_out)
        return self.add(eng, lambda e: e.activation(out, in_, func, **kw), reads=reads, writes=writes)

    def tt(self, out, in0, in1, op, eng="dve"):
        return self.add(eng, lambda e: e.tensor_tensor(out, in0, in1, op), reads=[in0, in1], writes=[out])

    def ts(self, out, in0, s1, op0, s2=None, op1=None, eng="dve", accum_out=None):
        reads = [in0]
        if not isinstance(s1, (int, float)):
            reads.append(s1)
        if s2 is not None and not isinstance(s2, (int, float)):
            reads.append(s2)
        writes = [out]
        kw = {}
        if accum_out is not None:
            kw["accum_out"] = accum_out
            writes.append(accum_out)
        if op1 is None:
            if accum_out is None:
                return self.add(eng, lambda e: e.tensor_single_scalar(out, in0, s1, op0), reads=reads, writes=writes)
            return self.add(eng, lambda e: e.tensor_scalar(out, in0, s1, None, op0, **kw), reads=reads, writes=writes)
        return self.add(eng, lambda e: e.tensor_scalar(out, in0, s1, s2, op0, op1, **kw), reads=reads, writes=writes)

    def stt(self, out, in0, scalar, in1, op0, op1, eng="dve"):
        reads = [in0, in1]
        if not isinstance(scalar, (int, float)):
            reads.append(scalar)
        return self.add(eng, lambda e: e.scalar_tensor_tensor(out, in0, scalar, in1, op0, op1),
                        reads=reads, writes=[out])

    def copy(self, out, in_, eng="dve"):
        if eng == "act":
            return self.add("act", lambda e: e.copy(out, in_), reads=[in_], writes=[out])
        return self.add(eng, lambda e: e.tensor_copy(out, in_), reads=[in_], writes=[out])

    def memset(self, ap, val, eng="pool"):
        return self.add(eng, lambda e: e.memset(ap, val), reads=[], writes=[ap])


import contextlib
from concourse.bass_utils import run_bass_kernel_spmd

D = 1024; S = 2048; NH = 8; DH = 64; DFF = 2816; INW = 3864
TB = 512; NT = TB // 128; NBLK = S // TB
BIG = 32768.0
EPS = 1e-6
TWO_PI = float(2 * np.pi)
PI = float(np.pi)
MAGIC = 12582912.0
C_U, C_Q, C_KV, C_G, C_MG = 0, 512, 1024, 1792, 1816
SCALE = DH ** -0.5


class _Stop(Exception):
    pass


def build(nseq, dbg=False, stop_after=99):
    nc = bass.Bass("TRN2", target_bir_lowering=False)
    P = Prog(nc)
    st = contextlib.ExitStack()

    def din(name, shape, dt=F32):
        P.notrack.add(name)
        return nc.dram_tensor(name, list(shape), dt, kind="ExternalInput").ap()

    def sb(name, shape, dt=F32):
        return st.enter_context(nc.sbuf_tensor(name, list(shape), dt))

    x = din("x", [nseq, S, D]); pos = din("pos", [nseq, S], I32)
    mix_g = din("mix_norm_g", [D]); w_in = din("w_in", [D, INW]); qg = din("q_norm_g", [DH])
    kg = din("k_norm_g", [3, DH]); cpe = din("cmp_pe", [2, 32, DH]); cw1 = din("cmp_w1", [2, 32, DH, 128])
    cw2 = din("cmp_w2", [2, 128, DH]); poolw = din("pool_w", [4, 128, 128]); pscale = din("pool_scale", [512])
    wpo = din("w_pool_out", [512, D]); wno = din("w_nsa_out", [512, D]); wout = din("w_out", [D, D])
    ffn_g = din("ffn_norm_g", [D]); wup = din("w_up", [D, 2 * DFF]); convw = din("conv_w", [3, DFF])
    convb = din("conv_b", [DFF]); wdn = din("w_down", [DFF, D])
    c_ident = din("c_ident", [128, 128]); c_mask = din("c_mask", [128, 2, 512]); c_cb8 = din("c_cb8", [8, 512])
    c_J = din("c_J", [8, 256]); c_EB = din("c_EB", [32, S]); c_ov = din("c_ov", [32, NBLK * 33])
    c_F = din("c_F", [128, 16, 32]); c_rope = din("c_rope", [64, 2]); c_perm = din("c_perm", [64, 64])
    c_invc = din("c_invc", [128, 4, 16])
    P.notrack.add("out")
    out = nc.dram_tensor("out", [nseq, S, D], F32, kind="ExternalOutput").ap()
    dbg_list = []

    def dump(name, ap, shape, dt=F32):
        if not dbg:
            return
        P.notrack.add(name)
        d = nc.dram_tensor(name, list(shape), dt, kind="ExternalOutput").ap()
        P.dma(d, ap, key="dbg_" + name, is_out=True)
        dbg_list.append(name)

    def stage(k):
        if k > stop_after:
            raise _Stop()

    with st:
        banks = [st.enter_context(nc.psum_tensor("pb%d" % i, [128, 512], F32)) for i in range(8)]
        rr = [0]
        bmode = [0]

        def bank():
            order = (0, 1, 2, 3, 7) if bmode[0] else (0, 1, 2, 3, 7, 4, 5, 6)
            i = order[rr[0] % len(order)]
            rr[0] += 1
            return banks[i]

        ident = sb("ident", [128, 128], BF16); identf = sb("identf", [128, 128])
        maskb = sb("maskb", [128, 2, 512], BF16); cb8 = sb("cb8", [8, 512], BF16); Jm = sb("Jm", [8, 256], BF16)
        EB = sb("EB", [32, S], BF16); ovT = sb("ovT", [32, NBLK * 33], BF16); Fc = sb("Fc", [128, 16, 32])
        ropec = sb("ropec", [64, 2]); perm = sb("perm", [64, 64], BF16); ones64 = sb("ones64", [64, 64], BF16)
        invc = sb("invc", [128, 4, 16]); epsb = sb("epsb", [128, 1])
        P.dma(ident[:], c_ident, key="c0", eng="pool"); P.dma(identf[:], c_ident, key="c1")
        P.dma(maskb[:], c_mask, key="c2", eng="pool"); P.dma(cb8[:], c_cb8, key="c3", eng="pool")
        P.dma(Jm[:], c_J, key="c4", eng="pool"); P.dma(EB[:], c_EB, key="c5", eng="pool")
        P.dma(ovT[:], c_ov, key="c6", eng="pool"); P.dma(Fc[:], c_F, key="c7")
        P.dma(ropec[:], c_rope, key="c8"); P.dma(perm[:], c_perm, key="c9", eng="pool")
        P.dma(invc[:], c_invc, key="c10")
        P.memset(ones64[:], 1.0); P.memset(epsb[:], EPS)
        xres = sb("xres", [128, NT, D])
        rows = xres[0:32].rearrange("p a b -> p (a b)")[:, 0:9 * 128].rearrange("p (i c) -> p i c", c=128)
        colsb = sb("colsb", [128, 9, 32])
        P.memset(rows, 0.0)
        P.dma(rows[0:8, 0, :], mix_g.rearrange("(a p) -> a p", p=128), key="r0")
        P.dma(rows[0:8, 1, :], ffn_g.rearrange("(a p) -> a p", p=128), key="r1")
        P.dma(rows[0:4, 2, :], pscale.rearrange("(a p) -> a p", p=128), key="r2")
        P.dma(rows[0:22, 3, :], convb.rearrange("(a p) -> a p", p=128), key="r3")
        for k in range(3):
            P.dma(rows[0:22, 4 + k, :], convw[k].rearrange("(a p) -> a p", p=128), key="r4")
        for kv in range(2):
            P.dma(rows[0:32, 7 + kv, 0:64], cpe[kv], key="r5")
        for i in range(9):
            pb = bank()
            P.transpose(pb[:, 0:32], rows[:, i, :], identf[0:32, 0:32])
            P.copy(colsb[:, i, :], pb[:, 0:32])
        gmix = colsb[:, 0, :]; gffn = colsb[:, 1, :]; pscol = colsb[:, 2, :]; convb_sb = colsb[:, 3, :]
        pes = sb("pes", [64, 2, 32], BF16)
        for kv in range(2):
            P.copy(pes[:, kv, :], colsb[0:64, 7 + kv, :])
        qgc = sb("qgc", [64, 1]); kgc = sb("kgc", [64, 3])
        P.dma(qgc[:], qg.rearrange("(d o) -> d o", o=1), key="r6")
        for k in range(3):
            P.dma(kgc[:, k:k + 1], kg[k].rearrange("(d o) -> d o", o=1), key="r7")
        w2k = sb("w2k", [128, 64], BF16); w2v = sb("w2v", [128, 64], BF16); poolw_sb = sb("poolw_sb", [128, 4, 128], BF16)
        P.dma(w2k[:], cw2[0], key="w2", eng="pool"); P.dma(w2v[:], cw2[1], key="w2", eng="pool")
        P.dma(poolw_sb[:], poolw.rearrange("g i o -> i g o"), key="pw", eng="pool")
        hb = sb("hb", [128, 2])

        hT = sb("hT", [128, 8, TB], BF16); mT = sb("mT", [128, 8, TB], BF16)
        qT = sb("qT", [64, NT, 8, 128], BF16); pp = sb("pp", [128, 4, TB], BF16); OT = sb("OT", [128, 4, TB], BF16)
        ksT = sb("ksT", [64, 2, S], BF16); kwT = sb("kwT", [64, 2, S], BF16)
        vs1 = sb("vs1", [128, 16, 2, 65], BF16); vw1 = sb("vw1", [128, 16, 2, 65], BF16)
        K2 = sb("K2", [64, 4, TB + 16], BF16); G16 = sb("G16", [64, 16, TB // 16 + 1], BF16)
        kcT = sb("kcT", [64, NBLK, 2, 32], BF16); vc1 = sb("vc1", [32, NBLK, 2, 65], BF16)
        Ctab = sb("Ctab", [64, TB]); Stab = sb("Stab", [64, TB])
        gtm = sb("gtm", [128, NT, 24]); utail = sb("utail", [128, 4, 16]); halo = sb("halo", [128, 22, 2])
        slots = [sb("slot%d" % i, [128, 8, 512], BF16) for i in range(3)]
        srr = [0]

        def slot():
            i = srr[0] % 3
            srr[0] += 1
            return slots[i], "ws%d" % i

        wdsb = sb("wdsb", [128, 4, D], BF16); actT = sb("actT", [128, 4, TB], BF16)
        hn = [sb("hn%d" % i, [128, D], BF16) for i in range(2)]; junk = hn[1]
        ssq = sb("ssq", [128, NT]); rstd = sb("rstd", [128, NT])
        ubuf = sb("ubuf", [128, TB + 16]); pa = sb("pa", [128, TB + 16]); pbf = sb("pbf", [128, TB + 16])
        pooled = sb("pooled", [128, TB], BF16)
        NR = [dict(sq=sb("t_sq%d" % i, [64, 512], BF16), sd=sb("t_sd%d" % i, [64, 512]), qn=sb("t_qn%d" % i, [64, 512]),
                   qnb=sb("t_qnb%d" % i, [64, 512], BF16), t2=sb("t_2%d" % i, [64, 512])) for i in range(2)]
        nrr = [0]
        posi = sb("posi", [64, TB], I32); ang = sb("ang", [64, TB]); ra = sb("ra", [64, TB]); rb = sb("rb", [64, TB])
        Eb = [sb("Eb%d" % i, [128, 512], BF16) for i in range(4)]
        Ec = [sb("Ec%d" % i, [32, 512], BF16) for i in range(NBLK)]
        hid = sb("hid", [128, 32], BF16)
        den = sb("den", [128, 3, 4]); fcm = sb("fcm", [128, 3, 4]); oacc = sb("oacc", [128, 4, 64]); otmp = sb("otmp", [128, 4, 64])
        Otm = sb("Otm", [128, 512], BF16)
        impt = sb("impt", [128, 4, 32]); imp = sb("imp", [128, 32]); imp2 = sb("imp2", [128, 32]); m8 = sb("m8", [128, 16])
        negb = sb("negb", [128, 32], BF16); negbT = sb("negbT", [32, 512], BF16); rci = sb("rci", [128, 4])
        gsg = [sb("gsg%d" % i, [128, TB], BF16) for i in range(4)]; tmpfs = [sb("tmpf%d" % i, [128, TB]) for i in range(2)]; tmpf = tmpfs[0]; mrr = [0]
        gbufs = [sb("gbuf%d" % i, [128, TB + 2]) for i in range(2)]; abufs = [sb("abuf%d" % i, [128, TB]) for i in range(2)]; frr = [0]
        if dbg:
            print("sbuf bytes remaining", nc.sbuf_bytes_remaining)
        P.memset(vs1[:], 1.0); P.memset(vw1[:], 1.0); P.memset(vc1[:], 1.0)
        erot = [0]

        w1buf = sb("w1buf", [64, 32, 128], BF16)

        for kv in range(2):
            w1s_ = w1buf[:]
            P.dma(w1s_, cw1[kv].rearrange("l d h -> d l h"), key="w1b", eng="pool")
            pb = bank()
            for l in range(32):
                P.mm(pb[:, 0:1], w1s_[:, l, :], pes[:, kv, l:l + 1], start=(l == 0), stop=(l == 31))
            P.copy(hb[:, kv:kv + 1], pb[:, 0:1])

        def wload(dst, src, key):
            P.dma(dst, src, key=key, eng="pool")

        def w_cols(wmat, c0, ncol):
            return wmat.rearrange("(kc p) c -> p kc c", p=128)[:, :, c0:c0 + ncol]

        def proj_fm(ps, wsl, c0, M):
            for kc in range(8):
                P.mm(ps[0:M, 0:TB], wsl[:, kc, c0:c0 + M], hT[:, kc, :], start=(kc == 0), stop=(kc == 7))

        def norm_T(gcol):
            for tt in range(NT):
                P.act(junk[:], xres[:, tt, :], AF.Square, accum_out=ssq[:, tt:tt + 1])
            P.act(rstd[:], ssq[:], AF.Sqrt, bias=epsb[:, 0:1], scale=1.0 / D)
            P.add("dve", lambda e: e.reciprocal(rstd[:], rstd[:]), reads=[rstd[:]], writes=[rstd[:]])
            for tt in range(NT):
                h_ = hn[tt % 2]
                P.ts(h_[:], xres[:, tt, :], rstd[:, tt:tt + 1], ALU.mult)
                pb = bank()
                pbb = pb[:].bitcast(BF16)
                for kc in range(8):
                    P.transpose(pbb[:, kc * 128:(kc + 1) * 128], h_[:, kc * 128:(kc + 1) * 128], ident[:])
                for kc in range(8):
                    P.ts(hT[:, kc, tt * 128:(tt + 1) * 128], pbb[:, kc * 128:(kc + 1) * 128], gcol[:, kc:kc + 1], ALU.mult)

        nrq = {"b": None, "c": None}

        def nr_flush():
            if nrq["b"] is not None:
                b_, c_ = nrq["b"]
                b_()
                if nrq["c"] is not None:
                    nrq["c"]()
                c_()
            elif nrq["c"] is not None:
                nrq["c"]()
            nrq["b"] = None; nrq["c"] = None

        def normrope(mkps, gvec, cs, ss, out_ap, n, out3=None):
            T = NR[nrr[0] % 2]
            nrr[0] += 1
            t_sq, t_sd, t_qn, t_qnb, t_2 = T["sq"], T["sd"], T["qn"], T["qnb"], T["t2"]
            ps = mkps()
            P.act(t_sq[:, 0:n], ps, AF.Square)

            def B():
                p2 = bank()
                P.mm(p2[0:64, 0:n], ones64[:], t_sq[:, 0:n])
                P.act(t_sd[:, 0:n], p2[0:64, 0:n], AF.Sqrt, bias=epsb[0:64, 0:1], scale=1.0 / DH)
                P.add("dve", lambda e: e.reciprocal(t_sd[:, 0:n], t_sd[:, 0:n]), reads=[t_sd[:, 0:n]], writes=[t_sd[:, 0:n]])
                P.stt(t_qn[:, 0:n], ps, gvec, t_sd[:, 0:n], ALU.mult, ALU.mult)
                P.copy(t_qnb[:, 0:n], t_qn[:, 0:n], eng="act")

            def C():
                p3 = bank()
                P.mm(p3[0:64, 0:n], perm[:], t_qnb[:, 0:n])
                P.tt(t_qn[:, 0:n], t_qn[:, 0:n], cs, ALU.mult)
                P.tt(t_2[:, 0:n], p3[0:64, 0:n], ss, ALU.mult)
                if out3 is not None:
                    P.tt(out3, t_qn[:, 0:n].rearrange("p (a b) -> p a b", b=128), t_2[:, 0:n].rearrange("p (a b) -> p a b", b=128), ALU.add)
                else:
                    P.tt(out_ap, t_qn[:, 0:n], t_2[:, 0:n], ALU.add)

            if nrq["b"] is not None:
                b_, c_ = nrq["b"]
                b_()
                if nrq["c"] is not None:
                    nrq["c"]()
                nrq["c"] = c_
            nrq["b"] = (B, C)

        def sincos(dst, shift, sign_col):
            P.ts(ra[:], ang[:], 1.0 / TWO_PI, ALU.mult, shift / TWO_PI, ALU.add)
            P.ts(ra[:], ra[:], MAGIC, ALU.add, MAGIC, ALU.subtract)
            P.stt(rb[:], ra[:], -TWO_PI, ang[:], ALU.mult, ALU.add)
            if shift != 0.0:
                P.ts(rb[:], rb[:], shift, ALU.add)
            P.ts(rb[:], rb[:], PI, ALU.min, -PI, ALU.max)
            P.act(ra[:], rb[:], AF.Sin)
            if sign_col is None:
                P.copy(dst, ra[:])
            else:
                P.ts(dst, ra[:], sign_col, ALU.mult)

        def exp_tile(ps, nk):
            e_ = Eb[erot[0] % 4]
            erot[0] += 1
            P.act(e_[0:nk, :], ps[0:nk, :], AF.Exp, scale=SCALE)
            return e_

        try:
          for s_i in range(nseq):
              P.memset(utail[:], 0.0); P.memset(halo[:], 0.0); P.memset(K2[:], 0.0)
              for blk in range(NBLK):
                  T0 = blk * TB
                  QT0 = blk * NT
                  for tt in range(NT):
                      P.dma(xres[:, tt, :], x[s_i, T0 + tt * 128:T0 + (tt + 1) * 128, :], key="x")
                  stage(1)
                  norm_T(gmix)
                  stage(2)
                  P.dma(posi[:], pos[s_i:s_i + 1, T0:T0 + TB].partition_broadcast(64), key="pos")
                  P.copy(rb[:], posi[:])
                  P.ts(ang[:], rb[:], ropec[:, 0:1], ALU.mult)
                  sincos(Stab[:], 0.0, ropec[:, 1:2])
                  sincos(Ctab[:], PI / 2, None)
                  stage(3)
                  sl, sk = slot(); wload(sl[:, :, 0:512], w_cols(w_in, C_U, 512), sk)
                  for gi, wdw in enumerate((2, 4, 8, 16)):
                      pb = bank()
                      proj_fm(pb, sl, gi * 128, 128)
                      P.copy(ubuf[:, 0:16], utail[:, gi, :])
                      P.copy(ubuf[:, 16:16 + TB], pb[:, 0:TB], eng="act")
                      P.copy(utail[:, gi, :], ubuf[:, TB:TB + 16])
                      src = ubuf; sh = 1; lo = 1; tog = 0
                      while sh < wdw:
                          dst = pa if tog == 0 else pbf
                          P.tt(dst[:, lo:TB + 16], src[:, lo:TB + 16], src[:, lo - sh:TB + 16 - sh], ALU.add)
                          src = dst; tog ^= 1; sh *= 2; lo = 2 * sh - 1
                      P.stt(pooled[:], src[:, 16:16 + TB], 1.0 / wdw, ubuf[:, 16:16 + TB], ALU.mult, ALU.subtract)
                      if blk == 0:
                          P.tt(tmpf[:, 0:16], src[:, 16:32], invc[:, gi, :], ALU.mult)
                          P.tt(pooled[:, 0:16], tmpf[:, 0:16], ubuf[:, 16:32], ALU.subtract)
                      pb2 = bank()
                      P.mm(pb2[:, 0:TB], poolw_sb[:, gi, :], pooled[:])
                      P.ts(pp[:, gi, :], pb2[:, 0:TB], pscol[:, gi:gi + 1], ALU.mult)
                  stage(4)
                  sl, sk = slot(); slv = sl[:].rearrange("p a b -> p (a b)").rearrange("p (g c) -> p g c", g=4)
                  wload(slv, wpo.rearrange("(g p) c -> p g c", p=128), sk)
                  for mc in range(8):
                      pb = bank()
                      for gi in range(4):
                          P.mm(pb[:, 0:TB], slv[:, gi, mc * 128:(mc + 1) * 128], pp[:, gi, :], start=(gi == 0), stop=(gi == 3))
                      P.copy(mT[:, mc, :], pb[:, 0:TB], eng="act")
                  stage(5)
                  sl, sk = slot(); wload(sl[:, :, 0:512], w_cols(w_in, C_Q, 512), sk)
                  for h in range(8):
                      def mk(h=h, sl=sl):
                          pb = bank()
                          proj_fm(pb, sl, h * 64, 64)
                          return pb[0:64, 0:TB]
                      normrope(mk, qgc[:, 0:1], Ctab[:], Stab[:], None, TB, out3=qT[:, :, h, :])
                  nr_flush()
                  stage(6)
                  sl, sk = slot(); wload(sl[:, :, 0:512], w_cols(w_in, C_KV, 512), sk)
                  sl2, sk2 = slot(); wload(sl2[:, :, 0:256], w_cols(w_in, C_KV + 512, 256), sk2)
                  wload(sl2[:, :, 256:280], w_cols(w_in, C_G, 24), sk2)
                  for g in range(2):
                      def mk1(g=g, sl=sl):
                          pb = bank(); proj_fm(pb, sl, 256 + g * 64, 64)
                          return pb[0:64, 0:TB]
                      normrope(mk1, kgc[:, 1:2], Ctab[:], Stab[:], ksT[:, g, T0:T0 + TB], TB)
                      def mk2(g=g, sl2=sl2):
                          pb = bank(); proj_fm(pb, sl2, g * 64, 64)
                          return pb[0:64, 0:TB]
                      normrope(mk2, kgc[:, 2:3], Ctab[:], Stab[:], kwT[:, g, T0:T0 + TB], TB)
                  nr_flush()
                  stage(7)
                  NB = TB // 16 - (1 if blk == 0 else 0)
                  base = 16 if blk == 0 else 0
                  ctab0 = 31 if blk == 0 else 15
                  for kv in range(2):
                      w1s_ = w1buf[:]
                      wload(w1s_, cw1[kv].rearrange("l d h -> d l h"), "w1b")
                      for g in range(2):
                          ki = kv * 2 + g
                          pb = bank(); proj_fm(pb, sl, kv * 128 + g * 64, 64)
                          if blk > 0:
                              P.copy(K2[:, ki, 0:16], K2[:, ki, TB:TB + 16])
                          P.copy(K2[:, ki, 16:16 + TB], pb[0:64, 0:TB], eng="act")
                          P.copy(G16[:, :, 0:NB + 1], K2[:, ki, base:base + 16 * (NB + 1)].rearrange("p (i j) -> p j i", j=16))
                          ph = bank()
                          for l in range(32):
                              P.mm(ph[:, 0:NB], w1s_[:, l, :], G16[:, l % 16, l // 16:l // 16 + NB], start=(l == 0), stop=(l == 31))
                          P.act(hid[:, 0:NB], ph[:, 0:NB], AF.Silu, bias=hb[:, kv:kv + 1])
                          if kv == 0:
                              def mk3(NB=NB):
                                  p4 = bank()
                                  P.mm(p4[0:64, 0:NB], w2k[:], hid[:, 0:NB])
                                  return p4[0:64, 0:NB]
                              cend = ctab0 + 16 * (NB - 1) + 1
                              normrope(mk3, kgc[:, 0:1], Ctab[:, ctab0:cend:16], Stab[:, ctab0:cend:16], kcT[:, blk, g, 0:NB], NB)
                              if g == 1:
                                  nr_flush()
                          else:
                              p4 = bank()
                              P.mm(p4[0:NB, 0:64], hid[:, 0:NB], w2v[:])
                              P.copy(vc1[0:NB, blk, g, 0:64], p4[0:NB, 0:64], eng="act")
                  stage(8)
                  for tt in range(NT):
                      pb = bank(); pc = bank()
                      for kc in range(8):
                          P.mm(pb[:, 0:128], hT[:, kc, tt * 128:(tt + 1) * 128], sl[:, kc, 384:512], start=(kc == 0), stop=(kc == 7))
                      for kc in range(8):
                          P.mm(pc[:, 0:152], hT[:, kc, tt * 128:(tt + 1) * 128], sl2[:, kc, 128:280], start=(kc == 0), stop=(kc == 7))
                      P.copy(vs1[:, QT0 + tt, :, 0:64], pb[:, 0:128].rearrange("p (g d) -> p g d", g=2), eng="act")
                      P.copy(vw1[:, QT0 + tt, :, 0:64], pc[:, 0:128].rearrange("p (g d) -> p g d", g=2), eng="act")
                      P.act(gtm[:, tt, :], pc[:, 128:152], AF.Sigmoid)
                  if dbg and blk == 0:
                      dump("d_qT", qT[:], [64, NT, 8, 128], BF16); dump("d_ksT", ksT[:, :, 0:TB], [64, 2, TB], BF16)
                      dump("d_kwT", kwT[:, :, 0:TB], [64, 2, TB], BF16)
                      dump("d_pp", pp[:], [128, 4, TB], BF16); dump("d_hT", hT[:], [128, 8, TB], BF16)
                      dump("d_kcT", kcT[:, 0, :, 0:31], [64, 2, 31], BF16); dump("d_vc1", vc1[0:31, 0, :, :], [31, 2, 65], BF16)
                      dump("d_gtm", gtm[:], [128, NT, 24]); dump("d_vs1", vs1[:, 0:NT, :, :], [128, NT, 2, 65], BF16)
                      dump("d_Ctab", Ctab[:], [64, TB]); dump("d_Stab", Stab[:], [64, TB]); dump("d_mT0", mT[:], [128, 8, TB], BF16)

                  stage(9)
                  bmode[0] = 1
                  pend = []

                  def pv_flush():
                      for fn in pend:
                          fn()
                      del pend[:]

                  for ql in range(NT):
                      qt = QT0 + ql
                      for g in range(2):
                          rhsQ = qT[:, ql, 4 * g:4 * g + 4, :].rearrange("p h q -> p (h q)")
                          Oc, Os, Ow = banks[4], banks[5], banks[6]
                          tiles = []
                          for b in range(blk + 1):
                              n0 = 0 if b == 0 else b * (TB // 16) - 1
                              nbb = TB // 16 - (1 if b == 0 else 0)
                              nk = min(nbb, 8 * qt + 7 - n0)
                              if nk <= 0:
                                  continue
                              tiles.append((b, n0, nk, b == blk))
                          for ti, (b, n0, nk, masked) in enumerate(tiles):
                              ps = bank()
                              P.mm(ps[0:nk, :], kcT[:, b, g, 0:nk], rhsQ, start=True, stop=not masked)
                              if masked:
                                  s0 = 121 - 8 * qt + n0
                                  P.mm(ps[0:nk, :], Jm[:, s0:s0 + nk], cb8[:], start=False, stop=True)
                              P.act(Ec[b][0:nk, :], ps[0:nk, :], AF.Exp, scale=SCALE)
                          for r in range(4):
                              for ti, (b, n0, nk, masked) in enumerate(tiles):
                                  P.mm(Oc[:, r * 128:r * 128 + 65], Ec[b][0:nk, r * 128:(r + 1) * 128], vc1[0:nk, b, g, :],
                                       start=(ti == 0 and r == 0), stop=(ti == len(tiles) - 1 and r == 3))
                          use_sel = qt >= 8
                          if use_sel:
                              pi_ = bank()
                              for r in range(4):
                                  for ti, (b, n0, nk, masked) in enumerate(tiles):
                                      P.mm(pi_[:, r * 64:r * 64 + 33], Ec[b][0:nk, r * 128:(r + 1) * 128], ovT[0:nk, b * 33:(b + 1) * 33],
                                           start=(ti == 0 and r == 0), stop=(ti == len(tiles) - 1 and r == 3))
                              piv = pi_[:, 0:256].rearrange("p (r c) -> p r c", c=64)
                              P.add("dve", lambda e, a=piv: e.reciprocal(rci[:], a[:, :, 32]), reads=[pi_[:, 0:256]], writes=[rci[:]])
                              P.tt(impt[:], piv[:, :, 0:32], rci[:].unsqueeze(2).to_broadcast([128, 4, 32]), ALU.mult)
                              P.add("dve", lambda e: e.tensor_reduce(imp[:], impt[:].rearrange("p r j -> p j r"), AX.X, ALU.add),
                                    reads=[impt[:]], writes=[imp[:]])
                              P.tt(imp2[:], imp[:], Fc[:, qt, :], ALU.max)
                              P.add("dve", lambda e: e.max(out=m8[:, 0:8], in_=imp2[:]), reads=[imp2[:]], writes=[m8[:, 0:8]])
                              P.add("dve", lambda e: e.match_replace(out=imp[:], in_to_replace=m8[:, 0:8], in_values=imp2[:], imm_value=-1e30),
                                    reads=[imp2[:], m8[:, 0:8]], writes=[imp[:]])
                              P.add("dve", lambda e: e.max(out=m8[:, 8:16], in_=imp[:]), reads=[imp[:]], writes=[m8[:, 8:16]])
                              P.ts(negb[:], imp2[:], m8[:, 15:16], ALU.is_lt, -BIG, ALU.mult)
                              pt = bank(); ptb = pt[:].bitcast(BF16)
                              for r in range(4):
                                  P.transpose(ptb[0:32, r * 128:(r + 1) * 128], negb[:], ident[:])
                              P.copy(negbT[:], ptb[0:32, 0:512])
                          for kt in range(qt + 1):
                              ps = bank()
                              last_plain = not (use_sel or kt == qt)
                              P.mm(ps[:, :], ksT[:, g, kt * 128:(kt + 1) * 128], rhsQ, start=True, stop=last_plain)
                              if use_sel:
                                  P.mm(ps[:, :], EB[:, kt * 128:(kt + 1) * 128], negbT[:], start=False, stop=(kt != qt))
                              if kt == qt:
                                  P.mm(ps[:, :], ident[:], maskb[:, 0, :], start=False, stop=True)
                              e_ = exp_tile(ps, 128)
                              pv_flush()

                              def pvs(e_=e_, kt=kt, g=g, qt=qt, Os=Os):
                                  for r in range(4):
                                      P.mm(Os[:, r * 128:r * 128 + 65], e_[:, r * 128:(r + 1) * 128], vs1[:, kt, g, :], start=(kt == 0 and r == 0), stop=(kt == qt and r == 3))
                              pend.append(pvs)
                          k0 = max(0, qt - 4)
                          for kt in range(k0, qt + 1):
                              ps = bank()
                              diag = kt == qt
                              bandt = kt == qt - 4
                              P.mm(ps[:, :], kwT[:, g, kt * 128:(kt + 1) * 128], rhsQ, start=True, stop=not (diag or bandt))
                              if diag:
                                  P.mm(ps[:, :], ident[:], maskb[:, 0, :], start=False, stop=True)
                              if bandt:
                                  P.mm(ps[:, :], ident[:], maskb[:, 1, :], start=False, stop=True)
                              e_ = exp_tile(ps, 128)
                              pv_flush()

                              def pvw(e_=e_, kt=kt, g=g, qt=qt, k0=k0, Ow=Ow):
                                  for r in range(4):
                                      P.mm(Ow[:, r * 128:r * 128 + 65], e_[:, r * 128:(r + 1) * 128], vw1[:, kt, g, :], start=(kt == k0 and r == 0), stop=(kt == qt and r == 3))
                              pend.append(pvw)
                          pv_flush()
                          for bi, Ob in enumerate((Oc, Os, Ow)):
                              ov_ = Ob[:, :].rearrange("p (r c) -> p r c", c=128)
                              P.ts(den[:, bi, :], ov_[:, :, 64], 1e-30, ALU.max)
                          P.add("dve", lambda e: e.reciprocal(den[:], den[:]), reads=[den[:]], writes=[den[:]])
                          gv = gtm[:, ql, 12 * g:12 * g + 12].rearrange("p (h b) -> p b h", b=3)
                          P.tt(fcm[:], den[:], gv, ALU.mult)
                          for bi, Ob in enumerate((Oc, Os, Ow)):
                              ov_ = Ob[:, :].rearrange("p (r c) -> p r c", c=128)
                              fb = fcm[:, bi, :].unsqueeze(2).to_broadcast([128, 4, 64])
                              if bi == 0:
                                  P.tt(oacc[:], ov_[:, :, 0:64], fb, ALU.mult)
                              else:
                                  P.tt(otmp[:], ov_[:, :, 0:64], fb, ALU.mult)
                                  dst = oacc[:] if bi == 1 else Otm[:, g * 256:(g + 1) * 256].rearrange("p (r d) -> p r d", d=64)
                                  P.tt(dst, oacc[:], otmp[:], ALU.add)
                      pt = bank(); ptb = pt[:].bitcast(BF16)
                      for c4 in range(4):
                          P.transpose(ptb[:, c4 * 128:(c4 + 1) * 128], Otm[:, c4 * 128:(c4 + 1) * 128], ident[:])
                      P.copy(OT[:, :, ql * 128:(ql + 1) * 128], ptb[:, 0:512].rearrange("p (c q) -> p c q", c=4), eng="act")
                  if dbg and blk == 0:
                      dump("d_OT", OT[:], [128, 4, TB], BF16)

                  stage(10)
                  bmode[0] = 0
                  for mb in range(2):
                      sln, skn = slot(); slnv = sln[:].rearrange("p a b -> p (a b)").rearrange("p (g c) -> p g c", g=4)
                      wload(slnv, wno.rearrange("(g p) c -> p g c", p=128), skn)
                      sgp, skp = slot(); wload(sgp[:, :, 0:512], w_cols(w_in, C_MG + mb * 512, 512), skp)
                      sgn, skg = slot(); wload(sgn[:, :, 0:512], w_cols(w_in, C_MG + 1024 + mb * 512, 512), skg)
                      for mi in range(4):
                          mc = mb * 4 + mi
                          p_gp = bank(); proj_fm(p_gp, sgp, mi * 128, 128)
                          p_gn = bank(); proj_fm(p_gn, sgn, mi * 128, 128)
                          p_yn = bank()
                          for c4 in range(4):
                              P.mm(p_yn[:, 0:TB], slnv[:, c4, mc * 128:(mc + 1) * 128], OT[:, c4, :], start=(c4 == 0), stop=(c4 == 3))
                          ga = gsg[2 * (mrr[0] % 2)]; gb = gsg[2 * (mrr[0] % 2) + 1]; tf = tmpfs[mrr[0] % 2]
                          mrr[0] += 1
                          P.act(ga[:], p_gp[:, 0:TB], AF.Sigmoid)
                          P.act(gb[:], p_gn[:, 0:TB], AF.Sigmoid)
                          P.tt(tf[:], p_yn[:, 0:TB], gb[:], ALU.mult)
                          P.tt(ga[:], ga[:], mT[:, mc, :], ALU.mult)
                          P.tt(mT[:, mc, :], ga[:], tf[:], ALU.add)
                  sa, ska = slot(); sbb, skb = slot()
                  wo_v = wout.rearrange("(kc p) c -> p kc c", p=128)
                  wload(sa[:, :, :], wo_v[:, :, 0:512], ska); wload(sbb[:, :, :], wo_v[:, :, 512:1024], skb)
                  for tt in range(NT):
                      for hh, sw in enumerate((sa, sbb)):
                          pb = bank()
                          for kc in range(8):
                              P.mm(pb[:, :], mT[:, kc, tt * 128:(tt + 1) * 128], sw[:, kc, :], start=(kc == 0), stop=(kc == 7))
                          P.tt(xres[:, tt, hh * 512:(hh + 1) * 512], xres[:, tt, hh * 512:(hh + 1) * 512], pb[:, :], ALU.add)
                  if dbg and blk == 0:
                      dump("d_x1", xres[:], [128, NT, D])

                  stage(11)
                  norm_T(gffn)
                  for f0 in range(0, 22, 4):
                      nf = min(4, 22 - f0)
                      wload(wdsb[:, 0:nf, :], wdn[f0 * 128:(f0 + nf) * 128, :].rearrange("(fc p) c -> p fc c", p=128), "wd")
                      sg_, skg_ = slot(); wload(sg_[:, :, 0:nf * 128], w_cols(wup, f0 * 128, nf * 128), skg_)
                      sv_, skv_ = slot(); wload(sv_[:, :, 0:nf * 128], w_cols(wup, DFF + f0 * 128, nf * 128), skv_)
                      for fi in range(nf):
                          f = f0 + fi
                          p_g = bank(); proj_fm(p_g, sg_, fi * 128, 128)
                          p_v = bank(); proj_fm(p_v, sv_, fi * 128, 128)
                          gbuf = gbufs[frr[0] % 2]; abuf = abufs[frr[0] % 2]
                          frr[0] += 1
                          P.copy(gbuf[:, 0:2], halo[:, f, :])
                          P.copy(gbuf[:, 2:2 + TB], p_g[:, 0:TB], eng="act")
                          P.copy(halo[:, f, :], gbuf[:, TB:TB + 2])
                          P.ts(abuf[:], gbuf[:, 2:2 + TB], colsb[:, 6, f:f + 1], ALU.mult, convb_sb[:, f:f + 1], ALU.add)
                          P.stt(abuf[:], gbuf[:, 1:1 + TB], colsb[:, 5, f:f + 1], abuf[:], ALU.mult, ALU.add)
                          P.stt(abuf[:], gbuf[:, 0:TB], colsb[:, 4, f:f + 1], abuf[:], ALU.mult, ALU.add)
                          P.act(abuf[:], abuf[:], AF.Silu)
                          P.tt(actT[:, fi, :], abuf[:], p_v[:, 0:TB], ALU.mult)
                      for tt in range(NT):
                          for hh in range(2):
                              pb = bank()
                              for fi in range(nf):
                                  P.mm(pb[:, :], actT[:, fi, tt * 128:(tt + 1) * 128], wdsb[:, fi, hh * 512:(hh + 1) * 512], start=(fi == 0), stop=(fi == nf - 1))
                              P.tt(xres[:, tt, hh * 512:(hh + 1) * 512], xres[:, tt, hh * 512:(hh + 1) * 512], pb[:, :], ALU.add)
                  for tt in range(NT):
                      P.dma(out[s_i, T0 + tt * 128:T0 + (tt + 1) * 128, :], xres[:, tt, :], key="o", is_out=True)
        except _Stop:
            for tt in range(NT):
                P.dma(out[0, tt * 128:(tt + 1) * 128, :], xres[:, tt, :], key="o", is_out=True)
        P.emit()
    return nc, P, dbg_list


def host_consts():
    c = {}
    c["c_ident"] = np.eye(128, dtype=np.float32)
    k = np.arange(128)[:, None]; q = np.arange(128)[None, :]
    caus = np.where(k <= q, 0.0, -BIG).astype(np.float32)
    band = np.where(k > q, 0.0, -BIG).astype(np.float32)
    c["c_mask"] = np.stack([np.tile(caus, (1, 4)), np.tile(band, (1, 4))], axis=1).astype(np.float32)
    i8 = np.arange(8)[:, None]
    cb = np.where(16 * i8 + 15 <= q, 0.0, -BIG).astype(np.float32)
    c["c_cb8"] = np.tile(cb, (1, 4)).astype(np.float32)
    J = np.zeros((8, 256), np.float32)
    for i in range(8):
        J[i, i + 120] = 1.0
    c["c_J"] = J
    EBm = np.zeros((32, S), np.float32)
    for j in range(32):
        EBm[j, j * 64:(j + 1) * 64] = 1.0
    c["c_EB"] = EBm
    n_cmp = 127
    sel_start = np.arange(32)[:, None] * 64
    cmp_start = np.arange(n_cmp)[None, :] * 16
    ov = np.clip(np.minimum(sel_start + 64, cmp_start + 32) - np.maximum(sel_start, cmp_start), 0, None) / 32.0
    ovfull = np.zeros((n_cmp, 33), np.float32); ovfull[:, 0:32] = ov.T; ovfull[:, 32] = 1.0
    ovt = np.zeros((NBLK, 32, 33), np.float32)
    for b in range(NBLK):
        n0 = 0 if b == 0 else b * (TB // 16) - 1
        nbb = TB // 16 - (1 if b == 0 else 0)
        ovt[b, 0:nbb] = ovfull[n0:n0 + nbb]
    c["c_ov"] = np.ascontiguousarray(ovt.transpose(1, 0, 2).reshape(32, NBLK * 33))
    F = np.zeros((16, 128, 32), np.float32)
    for qt in range(16):
        t = qt * 128 + np.arange(128)
        cur = t // 64
        for j in range(32):
            F[qt, :, j] = np.where((j == 0) | (j == cur) | (j == cur - 1), 1e9, 0.0)
    c["c_F"] = np.ascontiguousarray(F.transpose(1, 0, 2))
    rope = np.zeros((64, 2), np.float32)
    inv = (500000.0 ** (-(np.arange(8, dtype=np.float32) * 2.0 / 16))).astype(np.float32)
    for d in range(16):
        rope[d, 0] = inv[d % 8]
        rope[d, 1] = -1.0 if d < 8 else 1.0
    c["c_rope"] = rope
    pm = np.zeros((64, 64), np.float32)
    for m in range(16):
        pm[m + 8 if m < 8 else m - 8, m] = 1.0
    c["c_perm"] = pm
    invc = np.zeros((128, 4, 16), np.float32)
    for gi, w in enumerate((2, 4, 8, 16)):
        invc[:, gi, :] = 1.0 / np.minimum(np.arange(16) + 1.0, float(w))
    c["c_invc"] = invc
    return c


_CACHE = {}


def kernel(**inputs):
    n = 8
    B = inputs["x"].shape[0]
    nseq = B // n
    if "nc" not in _CACHE:
        _CACHE["nc"] = build(nseq)[0]
    nc = _CACHE["nc"]
    consts = host_consts()
    shared = {}
    for k_, v_ in inputs.items():
        if k_ in ("x", "positions"):
            continue
        a = np.ascontiguousarray(v_)
        shared[k_] = a[0] if a.shape[0] == 1 else a
    in_maps = []
    for c_ in range(n):
        m = dict(shared); m.update(consts)
        m["x"] = np.ascontiguousarray(inputs["x"][c_ * nseq:(c_ + 1) * nseq])
        m["pos"] = np.ascontiguousarray(inputs["positions"][c_ * nseq:(c_ + 1) * nseq]).astype(np.int32)
        in_maps.append(m)
    res = run_bass_kernel_spmd(nc, in_maps, core_ids=list(range(n)))
    return np.concatenate([r["out"] for r in res.results], axis=0).astype(np.float32)
```

```python
import numpy as np
import concourse.bass as bass
import concourse.mybir as mybir

F32 = mybir.dt.float32
BF16 = mybir.dt.bfloat16
I32 = mybir.dt.int32
AF = mybir.ActivationFunctionType
ALU = mybir.AluOpType
AX = mybir.AxisListType

_ESZ = {F32: 4, BF16: 2, I32: 4}


def esize(dt):
    if dt in _ESZ:
        return _ESZ[dt]
    s = str(dt)
    if '64' in s:
        return 8
    if '32' in s:
        return 4
    if '16' in s:
        return 2
    return 1


def bbox(ap):
    dims = ap.ap
    es = esize(ap.dtype)
    name = ap.tensor.name
    if name.startswith("pb"):
        return (name, 0, 128, 0, 2048)
    off = int(ap.offset)
    pstep, pcnt = dims[0]
    if pstep == 0:
        p0 = 0
        col = off
        pcnt_eff = 1
    else:
        p0 = off // pstep
        col = off % pstep
        pcnt_eff = pcnt
    ext = 0
    for st, cnt in dims[1:]:
        ext += abs(st) * (cnt - 1)
    return (name, p0, p0 + pcnt_eff, col * es, (col + ext + 1) * es)


class Op:
    __slots__ = ("eng", "fn", "deps", "dma_key", "idx", "marked", "mval", "waits", "dma_cnt")

    def __init__(self, eng, fn, dma_key):
        self.eng = eng
        self.fn = fn
        self.deps = set()
        self.dma_key = dma_key
        self.marked = False
        self.mval = 0
        self.waits = None
        self.dma_cnt = 0


ENGS = ("pe", "act", "dve", "pool", "sp")


class Prog:
    def __init__(self, nc):
        self.nc = nc
        self.ops = []
        self.live = {}
        self.dma_count = {}
        self.out_keys = set()
        self.notrack = set()

    def add(self, eng, fn, reads=(), writes=(), dma_key=None, is_out=False):
        op = Op(eng, fn, dma_key)
        op.idx = len(self.ops)
        self.ops.append(op)
        if dma_key is not None:
            c = self.dma_count.get(dma_key, 0) + 1
            self.dma_count[dma_key] = c
            op.dma_cnt = c
            if is_out:
                self.out_keys.add(dma_key)
        for ap in reads:
            self._access(op, ap, False)
        for ap in writes:
            self._access(op, ap, True)
        return op

    def _is_dma(self, op):
        return op.dma_key is not None

    def _access(self, op, ap, is_write):
        name, p0, p1, b0, b1 = bbox(ap)
        if name in self.notrack:
            return
        L = self.live.get(name)
        if L is None:
            L = []
            self.live[name] = L
        ops = self.ops
        newL = []
        for a in L:
            ap0, ap1, ab0, ab1, aw, aidx = a
            if aidx == op.idx:
                newL.append(a)
                continue
            if ap0 < p1 and p0 < ap1 and ab0 < b1 and b0 < ab1:
                prod = ops[aidx]
                if is_write or aw:
                    same = (prod.eng == op.eng) and not self._is_dma(prod) and not self._is_dma(op)
                    if same:
                        if op.eng != "pe":
                            op.deps.add(aidx)
                    else:
                        op.deps.add(aidx)
                contained = (p0 <= ap0 and ap1 <= p1 and b0 <= ab0 and ab1 <= b1)
                if contained:
                    if is_write:
                        continue
                    if (not aw) and prod.eng == op.eng and not self._is_dma(prod) and not self._is_dma(op):
                        continue
            newL.append(a)
        newL.append((p0, p1, b0, b1, is_write, op.idx))
        if len(newL) > 48:
            newL = self._compact(newL)
        self.live[name] = newL

    def _compact(self, L):
        ops = self.ops
        keep = []
        merged = {}
        for a in L:
            p0, p1, b0, b1, w, idx = a
            o = ops[idx]
            if w or self._is_dma(o):
                keep.append(a)
                continue
            m = merged.get(o.eng)
            if m is None:
                merged[o.eng] = [p0, p1, b0, b1, False, idx]
            else:
                m[0] = min(m[0], p0); m[1] = max(m[1], p1)
                m[2] = min(m[2], b0); m[3] = max(m[3], b1)
                m[5] = max(m[5], idx)
        for m in merged.values():
            keep.append(tuple(m))
        return keep

    def emit(self):
        nc = self.nc
        ops = self.ops
        for op in ops:
            best = {}
            keep = set()
            for d in op.deps:
                p = ops[d]
                if p.dma_key is not None:
                    keep.add(d)
                else:
                    if best.get(p.eng, -1) < d:
                        best[p.eng] = d
            keep.update(best.values())
            op.deps = keep
        for op in ops:
            for d in op.deps:
                ops[d].marked = True
        cnt = {e: 0 for e in ENGS}
        for op in ops:
            if op.dma_key is None and op.marked:
                cnt[op.eng] += 1
                op.mval = cnt[op.eng]
        seen = {e: {} for e in ENGS}
        dma_running = {}
        for op in ops:
            need = {}
            for d in op.deps:
                p = ops[d]
                if p.dma_key is not None:
                    key = ("dma", p.dma_key)
                    val = 16 * dma_running[p.dma_key]
                else:
                    key = ("eng", p.eng)
                    val = p.mval
                if need.get(key, 0) < val:
                    need[key] = val
            s = seen[op.eng]
            w = []
            for key, val in need.items():
                if s.get(key, 0) < val:
                    s[key] = val
                    w.append((key, val))
            op.waits = w
            if op.dma_key is not None:
                dma_running[op.dma_key] = op.dma_cnt
        import contextlib
        stack = contextlib.ExitStack()
        sems = {}
        for e in ENGS:
            sems[("eng", e)] = stack.enter_context(nc.semaphore("s_" + e))
        for k in self.dma_count:
            sems[("dma", k)] = stack.enter_context(nc.semaphore("d_" + str(k)))
        per_eng = {e: [op for op in ops if op.eng == e] for e in ENGS}
        finals = [(("dma", k), 16 * self.dma_count[k]) for k in self.out_keys]

        def run(engobj, elist, final=False):
            for op in elist:
                for key, val in op.waits:
                    engobj.wait_ge(sems[key], val)
                ins = op.fn(engobj)
                if op.dma_key is not None:
                    ins.then_inc(sems[("dma", op.dma_key)], 16)
                elif op.marked:
                    ins.then_inc(sems[("eng", op.eng)], 1)
            if final:
                for key, val in finals:
                    engobj.wait_ge(sems[key], val)

        with stack:
            with nc.Block() as block:
                @block.tensor
                def _(e):
                    run(e, per_eng["pe"])

                @block.scalar
                def _(e):
                    run(e, per_eng["act"])

                @block.vector
                def _(e):
                    run(e, per_eng["dve"])

                @block.gpsimd
                def _(e):
                    run(e, per_eng["pool"])

                @block.sync
                def _(e):
                    run(e, per_eng["sp"], final=True)
        self.stats = {e: len(per_eng[e]) for e in ENGS}
        self.stats["sem_max"] = dict(cnt)
        self.stats["nsem"] = len(sems)
        return nc

    def dma(self, out, in_, key, eng="sp", is_out=False, **kw):
        return self.add(eng, lambda e: e.dma_start(out=out, in_=in_, **kw),
                        reads=[in_], writes=[out], dma_key=key, is_out=is_out)

    def mm(self, out, lhsT, rhs, start=True, stop=True, **kw):
        return self.add("pe", lambda e: e.matmul(out, lhsT, rhs, start=start, stop=stop, **kw),
                        reads=[lhsT, rhs], writes=[out])

    def transpose(self, out, in_, ident):
        return self.add("pe", lambda e: e.transpose(out, in_, ident),
                        reads=[in_, ident], writes=[out])

    def act(self, out, in_, func, bias=None, scale=None, accum_out=None, eng="act"):
        reads = [in_]
        kw = {}
        if bias is not None:
            kw["bias"] = bias
            if not isinstance(bias, (int, float)):
                reads.append(bias)
        if scale is not None:
            kw["scale"] = scale
            if not isinstance(scale, (int, float)):
                reads.append(scale)
        writes = [out]
        if accum_out is not None:
            kw["accum_out"] = accum_out
            writes.append(accum_out)
        return self.add(eng, lambda e: e.activation(out, in_, func, **kw), reads=reads, writes=writes)

    def tt(self, out, in0, in1, op, eng="dve"):
        return self.add(eng, lambda e: e.tensor_tensor(out, in0, in1, op), reads=[in0, in1], writes=[out])

    def ts(self, out, in0, s1, op0, s2=None, op1=None, eng="dve", accum_out=None):
        reads = [in0]
        if not isinstance(s1, (int, float)):
            reads.append(s1)
        if s2 is not None and not isinstance(s2, (int, float)):
            reads.append(s2)
        writes = [out]
        kw = {}
        if accum_out is not None:
            kw["accum_out"] = accum_out
            writes.append(accum_out)
        if op1 is None:
            if accum_out is None:
                return self.add(eng, lambda e: e.tensor_single_scalar(out, in0, s1, op0), reads=reads, writes=writes)
            return self.add(eng, lambda e: e.tensor_scalar(out, in0, s1, None, op0, **kw), reads=reads, writes=writes)
        return self.add(eng, lambda e: e.tensor_scalar(out, in0, s1, s2, op0, op1, **kw), reads=reads, writes=writes)

    def stt(self, out, in0, scalar, in1, op0, op1, eng="dve"):
        reads = [in0, in1]
        if not isinstance(scalar, (int, float)):
            reads.append(scalar)
        return self.add(eng, lambda e: e.scalar_tensor_tensor(out, in0, scalar, in1, op0, op1),
                        reads=reads, writes=[out])

    def copy(self, out, in_, eng="dve"):
        if eng == "act":
            return self.add("act", lambda e: e.copy(out, in_), reads=[in_], writes=[out])
        return self.add(eng, lambda e: e.tensor_copy(out, in_), reads=[in_], writes=[out])

    def memset(self, ap, val, eng="pool"):
        return self.add(eng, lambda e: e.memset(ap, val), reads=[], writes=[ap])


import contextlib
from concourse.bass_utils import run_bass_kernel_spmd

D = 1024; S = 2048; NH = 8; DH = 64; DFF = 2816; INW = 3864
TB = 512; NT = TB // 128; NBLK = S // TB
BIG = 32768.0
EPS = 1e-6
TWO_PI = float(2 * np.pi)
PI = float(np.pi)
MAGIC = 12582912.0
C_U, C_Q, C_KV, C_G, C_MG = 0, 512, 1024, 1792, 1816
SCALE = DH ** -0.5


class _Stop(Exception):
    pass


def build(nseq, dbg=False, stop_after=99):
    nc = bass.Bass("TRN2", target_bir_lowering=False)
    P = Prog(nc)
    st = contextlib.ExitStack()

    def din(name, shape, dt=F32):
        P.notrack.add(name)
        return nc.dram_tensor(name, list(shape), dt, kind="ExternalInput").ap()

    def sb(name, shape, dt=F32):
        return st.enter_context(nc.sbuf_tensor(name, list(shape), dt))

    x = din("x", [nseq, S, D]); pos = din("pos", [nseq, S], I32)
    mix_g = din("mix_norm_g", [D]); w_in = din("w_in", [D, INW]); qg = din("q_norm_g", [DH])
    kg = din("k_norm_g", [3, DH]); cpe = din("cmp_pe", [2, 32, DH]); cw1 = din("cmp_w1", [2, 32, DH, 128])
    cw2 = din("cmp_w2", [2, 128, DH]); poolw = din("pool_w", [4, 128, 128]); pscale = din("pool_scale", [512])
    wpo = din("w_pool_out", [512, D]); wno = din("w_nsa_out", [512, D]); wout = din("w_out", [D, D])
    ffn_g = din("ffn_norm_g", [D]); wup = din("w_up", [D, 2 * DFF]); convw = din("conv_w", [3, DFF])
    convb = din("conv_b", [DFF]); wdn = din("w_down", [DFF, D])
    c_ident = din("c_ident", [128, 128]); c_mask = din("c_mask", [128, 2, 512]); c_cb8 = din("c_cb8", [8, 512])
    c_J = din("c_J", [8, 256]); c_EB = din("c_EB", [32, S]); c_ov = din("c_ov", [32, NBLK * 33])
    c_F = din("c_F", [128, 16, 32]); c_rope = din("c_rope", [64, 2]); c_perm = din("c_perm", [64, 64])
    c_invc = din("c_invc", [128, 4, 16])
    P.notrack.add("out")
    out = nc.dram_tensor("out", [nseq, S, D], F32, kind="ExternalOutput").ap()
    dbg_list = []

    def dump(name, ap, shape, dt=F32):
        if not dbg:
            return
        P.notrack.add(name)
        d = nc.dram_tensor(name, list(shape), dt, kind="ExternalOutput").ap()
        P.dma(d, ap, key="dbg_" + name, is_out=True)
        dbg_list.append(name)

    def stage(k):
        if k > stop_after:
            raise _Stop()

    with st:
        banks = [st.enter_context(nc.psum_tensor("pb%d" % i, [128, 512], F32)) for i in range(8)]
        rr = [0]
        bmode = [0]

        def bank(excl=()):
            order = (0, 1, 2, 3, 7) if bmode[0] else (0, 1, 2, 3, 7, 4, 5, 6)
            while True:
                i = order[rr[0] % len(order)]
                rr[0] += 1
                if i not in excl:
                    return banks[i]

        def bank_id(b):
            return int(b.name[2:])

        ident = sb("ident", [128, 128], BF16); identf = sb("identf", [128, 128])
        maskb = sb("maskb", [128, 2, 512], BF16); cb8 = sb("cb8", [8, 512], BF16); Jm = sb("Jm", [8, 256], BF16)
        EB = sb("EB", [32, S], BF16); ovT = sb("ovT", [32, NBLK * 33], BF16); Fc = sb("Fc", [128, 16, 32])
        ropec = sb("ropec", [64, 2]); perm = sb("perm", [64, 64], BF16); ones64 = sb("ones64", [64, 64], BF16)
        invc = sb("invc", [128, 4, 16]); epsb = sb("epsb", [128, 1])
        P.dma(ident[:], c_ident, key="c0", eng="pool"); P.dma(identf[:], c_ident, key="c1")
        P.dma(maskb[:], c_mask, key="c2", eng="pool"); P.dma(cb8[:], c_cb8, key="c3", eng="pool")
        P.dma(Jm[:], c_J, key="c4", eng="pool"); P.dma(EB[:], c_EB, key="c5", eng="pool")
        P.dma(ovT[:], c_ov, key="c6", eng="pool"); P.dma(Fc[:], c_F, key="c7")
        P.dma(ropec[:], c_rope, key="c8"); P.dma(perm[:], c_perm, key="c9", eng="pool")
        P.dma(invc[:], c_invc, key="c10")
        P.memset(ones64[:], 1.0); P.memset(epsb[:], EPS)
        xres = sb("xres", [128, NT, D])
        rows = xres[0:32].rearrange("p a b -> p (a b)")[:, 0:9 * 128].rearrange("p (i c) -> p i c", c=128)
        colsb = sb("colsb", [128, 9, 32])
        P.memset(rows, 0.0)
        P.dma(rows[0:8, 0, :], mix_g.rearrange("(a p) -> a p", p=128), key="r0")
        P.dma(rows[0:8, 1, :], ffn_g.rearrange("(a p) -> a p", p=128), key="r1")
        P.dma(rows[0:4, 2, :], pscale.rearrange("(a p) -> a p", p=128), key="r2")
        P.dma(rows[0:22, 3, :], convb.rearrange("(a p) -> a p", p=128), key="r3")
        for k in range(3):
            P.dma(rows[0:22, 4 + k, :], convw[k].rearrange("(a p) -> a p", p=128), key="r4")
        for kv in range(2):
            P.dma(rows[0:32, 7 + kv, 0:64], cpe[kv], key="r5")
        for i in range(9):
            pb = bank()
            P.transpose(pb[:, 0:32], rows[:, i, :], identf[0:32, 0:32])
            P.copy(colsb[:, i, :], pb[:, 0:32])
        gmix = colsb[:, 0, :]; gffn = colsb[:, 1, :]; pscol = colsb[:, 2, :]; convb_sb = colsb[:, 3, :]
        pes = sb("pes", [64, 2, 32], BF16)
        for kv in range(2):
            P.copy(pes[:, kv, :], colsb[0:64, 7 + kv, :])
        qgc = sb("qgc", [64, 1]); kgc = sb("kgc", [64, 3])
        P.dma(qgc[:], qg.rearrange("(d o) -> d o", o=1), key="r6")
        for k in range(3):
            P.dma(kgc[:, k:k + 1], kg[k].rearrange("(d o) -> d o", o=1), key="r7")
        w2k = sb("w2k", [128, 64], BF16); w2v = sb("w2v", [128, 64], BF16); poolw_sb = sb("poolw_sb", [128, 4, 128], BF16)
        P.dma(w2k[:], cw2[0], key="w2", eng="pool"); P.dma(w2v[:], cw2[1], key="w2", eng="pool")
        P.dma(poolw_sb[:], poolw.rearrange("g i o -> i g o"), key="pw", eng="pool")
        hb = sb("hb", [128, 2])

        hT = sb("hT", [128, 8, TB], BF16); mT = sb("mT", [128, 8, TB], BF16)
        qT = sb("qT", [64, NT, 8, 128], BF16); pp = sb("pp", [128, 4, TB], BF16); OT = sb("OT", [128, 4, TB], BF16)
        ksT = sb("ksT", [64, 2, S], BF16); kwT = sb("kwT", [64, 2, S], BF16)
        vs1 = sb("vs1", [128, 16, 2, 65], BF16); vw1 = sb("vw1", [128, 16, 2, 65], BF16)
        K2 = sb("K2", [64, 4, TB + 16], BF16); G16 = sb("G16", [64, 16, TB // 16 + 1], BF16)
        kcT = sb("kcT", [64, NBLK, 2, 32], BF16); vc1 = sb("vc1", [32, NBLK, 2, 65], BF16)
        Ctab = sb("Ctab", [64, TB]); Stab = sb("Stab", [64, TB])
        gtm = sb("gtm", [128, NT, 24]); utail = sb("utail", [128, 4, 16]); halo = sb("halo", [128, 22, 2])
        slots = [sb("slot%d" % i, [128, 8, 512], BF16) for i in range(3)]
        srr = [0]

        def slot():
            i = srr[0] % 3
            srr[0] += 1
            return slots[i], "ws%d" % i

        wdsb = sb("wdsb", [128, 4, D], BF16); actT = sb("actT", [128, 4, TB], BF16)
        hn = [sb("hn%d" % i, [128, D], BF16) for i in range(2)]; junk = hn[1]
        ssq = sb("ssq", [128, NT]); rstd = sb("rstd", [128, NT])
        ubuf = sb("ubuf", [128, TB + 16]); pa = sb("pa", [128, TB + 16]); pbf = sb("pbf", [128, TB + 16])
        pooled = sb("pooled", [128, TB], BF16)
        NR = [dict(sq=sb("t_sq%d" % i, [64, 512], BF16), sd=sb("t_sd%d" % i, [64, 512]), qn=sb("t_qn%d" % i, [64, 512]),
                   qnb=sb("t_qnb%d" % i, [64, 512], BF16), t2=sb("t_2%d" % i, [64, 512])) for i in range(2)]
        nrr = [0]
        posi = sb("posi", [64, TB], I32); ang = sb("ang", [64, TB]); ra = sb("ra", [64, TB]); rb = sb("rb", [64, TB])
        Eb = [sb("Eb%d" % i, [128, 512], BF16) for i in range(4)]
        Ec = [sb("Ec%d" % i, [32, 512], BF16) for i in range(NBLK)]
        hid = sb("hid", [128, 32], BF16)
        den = sb("den", [128, 3, 4]); fcm = sb("fcm", [128, 3, 4]); oacc = sb("oacc", [128, 4, 64]); otmp = sb("otmp", [128, 4, 64])
        Otm = sb("Otm", [128, 512], BF16)
        impt = sb("impt", [128, 4, 32]); imp = sb("imp", [128, 32]); imp2 = sb("imp2", [128, 32]); m8 = sb("m8", [128, 16])
        negb = sb("negb", [128, 32], BF16); negbT = sb("negbT", [32, 512], BF16); rci = sb("rci", [128, 4])
        gsg = [sb("gsg%d" % i, [128, TB], BF16) for i in range(4)]; tmpfs = [sb("tmpf%d" % i, [128, TB]) for i in range(2)]; tmpf = tmpfs[0]; mrr = [0]
        gbufs = [sb("gbuf%d" % i, [128, TB + 2]) for i in range(2)]; abufs = [sb("abuf%d" % i, [128, TB]) for i in range(2)]; frr = [0]
        if dbg:
            print("sbuf bytes remaining", nc.sbuf_bytes_remaining)
        P.memset(vs1[:], 1.0); P.memset(vw1[:], 1.0); P.memset(vc1[:], 1.0)
        erot = [0]

        w1buf = sb("w1buf", [64, 32, 128], BF16)

        for kv in range(2):
            w1s_ = w1buf[:]
            P.dma(w1s_, cw1[kv].rearrange("l d h -> d l h"), key="w1b", eng="pool")
            pb = bank()
            for l in range(32):
                P.mm(pb[:, 0:1], w1s_[:, l, :], pes[:, kv, l:l + 1], start=(l == 0), stop=(l == 31))
            P.copy(hb[:, kv:kv + 1], pb[:, 0:1])

        def wload(dst, src, key):
            P.dma(dst, src, key=key, eng="pool")

        def w_cols(wmat, c0, ncol):
            return wmat.rearrange("(kc p) c -> p kc c", p=128)[:, :, c0:c0 + ncol]

        def proj_fm(ps, wsl, c0, M):
            for kc in range(8):
                P.mm(ps[0:M, 0:TB], wsl[:, kc, c0:c0 + M], hT[:, kc, :], start=(kc == 0), stop=(kc == 7))

        def norm_T(gcol):
            for tt in range(NT):
                P.act(junk[:], xres[:, tt, :], AF.Square, accum_out=ssq[:, tt:tt + 1])
            P.act(rstd[:], ssq[:], AF.Sqrt, bias=epsb[:, 0:1], scale=1.0 / D)
            P.add("dve", lambda e: e.reciprocal(rstd[:], rstd[:]), reads=[rstd[:]], writes=[rstd[:]])
            for tt in range(NT):
                h_ = hn[tt % 2]
                P.ts(h_[:], xres[:, tt, :], rstd[:, tt:tt + 1], ALU.mult)
                pb = bank()
                pbb = pb[:].bitcast(BF16)
                for kc in range(8):
                    P.transpose(pbb[:, kc * 128:(kc + 1) * 128], h_[:, kc * 128:(kc + 1) * 128], ident[:])
                for kc in range(8):
                    P.ts(hT[:, kc, tt * 128:(tt + 1) * 128], pbb[:, kc * 128:(kc + 1) * 128], gcol[:, kc:kc + 1], ALU.mult)

        nrq = {"b": None, "c": None}

        def nr_flush():
            if nrq["b"] is not None:
                b_, c_ = nrq["b"]
                b_()
                if nrq["c"] is not None:
                    nrq["c"]()
                c_()
            elif nrq["c"] is not None:
                nrq["c"]()
            nrq["b"] = None; nrq["c"] = None

        def normrope(mkps, gvec, cs, ss, out_ap, n, out3=None):
            T = NR[nrr[0] % 2]
            nrr[0] += 1
            t_sq, t_sd, t_qn, t_qnb, t_2 = T["sq"], T["sd"], T["qn"], T["qnb"], T["t2"]
            ps = mkps()
            P.act(t_sq[:, 0:n], ps, AF.Square)

            def B():
                p2 = bank()
                P.mm(p2[0:64, 0:n], ones64[:], t_sq[:, 0:n])
                P.act(t_sd[:, 0:n], p2[0:64, 0:n], AF.Sqrt, bias=epsb[0:64, 0:1], scale=1.0 / DH)
                P.add("dve", lambda e: e.reciprocal(t_sd[:, 0:n], t_sd[:, 0:n]), reads=[t_sd[:, 0:n]], writes=[t_sd[:, 0:n]])
                P.stt(t_qn[:, 0:n], ps, gvec, t_sd[:, 0:n], ALU.mult, ALU.mult)
                P.copy(t_qnb[:, 0:n], t_qn[:, 0:n], eng="act")

            def C():
                p3 = bank()
                P.mm(p3[0:64, 0:n], perm[:], t_qnb[:, 0:n])
                P.tt(t_qn[:, 0:n], t_qn[:, 0:n], cs, ALU.mult)
                P.tt(t_2[:, 0:n], p3[0:64, 0:n], ss, ALU.mult)
                if out3 is not None:
                    P.tt(out3, t_qn[:, 0:n].rearrange("p (a b) -> p a b", b=128), t_2[:, 0:n].rearrange("p (a b) -> p a b", b=128), ALU.add)
                else:
                    P.tt(out_ap, t_qn[:, 0:n], t_2[:, 0:n], ALU.add)

            if nrq["b"] is not None:
                b_, c_ = nrq["b"]
                b_()
                if nrq["c"] is not None:
                    nrq["c"]()
                nrq["c"] = c_
            nrq["b"] = (B, C)

        def sincos(dst, shift, sign_col):
            P.ts(ra[:], ang[:], 1.0 / TWO_PI, ALU.mult, shift / TWO_PI, ALU.add)
            P.ts(ra[:], ra[:], MAGIC, ALU.add, MAGIC, ALU.subtract)
            P.stt(rb[:], ra[:], -TWO_PI, ang[:], ALU.mult, ALU.add)
            if shift != 0.0:
                P.ts(rb[:], rb[:], shift, ALU.add)
            P.ts(rb[:], rb[:], PI, ALU.min, -PI, ALU.max)
            P.act(ra[:], rb[:], AF.Sin)
            if sign_col is None:
                P.copy(dst, ra[:])
            else:
                P.ts(dst, ra[:], sign_col, ALU.mult)

        def exp_tile(ps, nk):
            e_ = Eb[erot[0] % 4]
            erot[0] += 1
            P.act(e_[0:nk, :], ps[0:nk, :], AF.Exp, scale=SCALE)
            return e_

        try:
          for s_i in range(nseq):
              P.memset(utail[:], 0.0); P.memset(halo[:], 0.0); P.memset(K2[:], 0.0)
              for blk in range(NBLK):
                  T0 = blk * TB
                  QT0 = blk * NT
                  for tt in range(NT):
                      P.dma(xres[:, tt, :], x[s_i, T0 + tt * 128:T0 + (tt + 1) * 128, :], key="x")
                  stage(1)
                  norm_T(gmix)
                  stage(2)
                  P.dma(posi[:], pos[s_i:s_i + 1, T0:T0 + TB].partition_broadcast(64), key="pos")
                  P.copy(rb[:], posi[:])
                  P.ts(ang[:], rb[:], ropec[:, 0:1], ALU.mult)
                  sincos(Stab[:], 0.0, ropec[:, 1:2])
                  sincos(Ctab[:], PI / 2, None)
                  stage(3)
                  sl, sk = slot(); wload(sl[:, :, 0:512], w_cols(w_in, C_U, 512), sk)
                  for gi, wdw in enumerate((2, 4, 8, 16)):
                      pb = bank()
                      proj_fm(pb, sl, gi * 128, 128)
                      P.copy(ubuf[:, 0:16], utail[:, gi, :])
                      P.copy(ubuf[:, 16:16 + TB], pb[:, 0:TB], eng="act")
                      P.copy(utail[:, gi, :], ubuf[:, TB:TB + 16])
                      src = ubuf; sh = 1; lo = 1; tog = 0
                      while sh < wdw:
                          dst = pa if tog == 0 else pbf
                          P.tt(dst[:, lo:TB + 16], src[:, lo:TB + 16], src[:, lo - sh:TB + 16 - sh], ALU.add)
                          src = dst; tog ^= 1; sh *= 2; lo = 2 * sh - 1
                      P.stt(pooled[:], src[:, 16:16 + TB], 1.0 / wdw, ubuf[:, 16:16 + TB], ALU.mult, ALU.subtract)
                      if blk == 0:
                          P.tt(tmpf[:, 0:16], src[:, 16:32], invc[:, gi, :], ALU.mult)
                          P.tt(pooled[:, 0:16], tmpf[:, 0:16], ubuf[:, 16:32], ALU.subtract)
                      pb2 = bank()
                      P.mm(pb2[:, 0:TB], poolw_sb[:, gi, :], pooled[:])
                      P.ts(pp[:, gi, :], pb2[:, 0:TB], pscol[:, gi:gi + 1], ALU.mult)
                  stage(4)
                  sl, sk = slot(); slv = sl[:].rearrange("p a b -> p (a b)").rearrange("p (g c) -> p g c", g=4)
                  wload(slv, wpo.rearrange("(g p) c -> p g c", p=128), sk)
                  for mc in range(8):
                      pb = bank()
                      for gi in range(4):
                          P.mm(pb[:, 0:TB], slv[:, gi, mc * 128:(mc + 1) * 128], pp[:, gi, :], start=(gi == 0), stop=(gi == 3))
                      P.copy(mT[:, mc, :], pb[:, 0:TB], eng="act")
                  stage(5)
                  sl, sk = slot(); wload(sl[:, :, 0:512], w_cols(w_in, C_Q, 512), sk)
                  for h in range(8):
                      def mk(h=h, sl=sl):
                          pb = bank()
                          proj_fm(pb, sl, h * 64, 64)
                          return pb[0:64, 0:TB]
                      normrope(mk, qgc[:, 0:1], Ctab[:], Stab[:], None, TB, out3=qT[:, :, h, :])
                  nr_flush()
                  stage(6)
                  sl, sk = slot(); wload(sl[:, :, 0:512], w_cols(w_in, C_KV, 512), sk)
                  sl2, sk2 = slot(); wload(sl2[:, :, 0:256], w_cols(w_in, C_KV + 512, 256), sk2)
                  wload(sl2[:, :, 256:280], w_cols(w_in, C_G, 24), sk2)
                  for g in range(2):
                      def mk1(g=g, sl=sl):
                          pb = bank(); proj_fm(pb, sl, 256 + g * 64, 64)
                          return pb[0:64, 0:TB]
                      normrope(mk1, kgc[:, 1:2], Ctab[:], Stab[:], ksT[:, g, T0:T0 + TB], TB)
                      def mk2(g=g, sl2=sl2):
                          pb = bank(); proj_fm(pb, sl2, g * 64, 64)
                          return pb[0:64, 0:TB]
                      normrope(mk2, kgc[:, 2:3], Ctab[:], Stab[:], kwT[:, g, T0:T0 + TB], TB)
                  nr_flush()
                  stage(7)
                  NB = TB // 16 - (1 if blk == 0 else 0)
                  base = 16 if blk == 0 else 0
                  ctab0 = 31 if blk == 0 else 15
                  for kv in range(2):
                      w1s_ = w1buf[:]
                      wload(w1s_, cw1[kv].rearrange("l d h -> d l h"), "w1b")
                      for g in range(2):
                          ki = kv * 2 + g
                          pb = bank(); proj_fm(pb, sl, kv * 128 + g * 64, 64)
                          if blk > 0:
                              P.copy(K2[:, ki, 0:16], K2[:, ki, TB:TB + 16])
                          P.copy(K2[:, ki, 16:16 + TB], pb[0:64, 0:TB], eng="act")
                          P.copy(G16[:, :, 0:NB + 1], K2[:, ki, base:base + 16 * (NB + 1)].rearrange("p (i j) -> p j i", j=16))
                          ph = bank()
                          for l in range(32):
                              P.mm(ph[:, 0:NB], w1s_[:, l, :], G16[:, l % 16, l // 16:l // 16 + NB], start=(l == 0), stop=(l == 31))
                          P.act(hid[:, 0:NB], ph[:, 0:NB], AF.Silu, bias=hb[:, kv:kv + 1])
                          if kv == 0:
                              def mk3(NB=NB):
                                  p4 = bank()
                                  P.mm(p4[0:64, 0:NB], w2k[:], hid[:, 0:NB])
                                  return p4[0:64, 0:NB]
                              cend = ctab0 + 16 * (NB - 1) + 1
                              normrope(mk3, kgc[:, 0:1], Ctab[:, ctab0:cend:16], Stab[:, ctab0:cend:16], kcT[:, blk, g, 0:NB], NB)
                              if g == 1:
                                  nr_flush()
                          else:
                              p4 = bank()
                              P.mm(p4[0:NB, 0:64], hid[:, 0:NB], w2v[:])
                              P.copy(vc1[0:NB, blk, g, 0:64], p4[0:NB, 0:64], eng="act")
                  stage(8)
                  for tt in range(NT):
                      pb = bank(); pc = bank()
                      for kc in range(8):
                          P.mm(pb[:, 0:128], hT[:, kc, tt * 128:(tt + 1) * 128], sl[:, kc, 384:512], start=(kc == 0), stop=(kc == 7))
                      for kc in range(8):
                          P.mm(pc[:, 0:152], hT[:, kc, tt * 128:(tt + 1) * 128], sl2[:, kc, 128:280], start=(kc == 0), stop=(kc == 7))
                      P.copy(vs1[:, QT0 + tt, :, 0:64], pb[:, 0:128].rearrange("p (g d) -> p g d", g=2), eng="act")
                      P.copy(vw1[:, QT0 + tt, :, 0:64], pc[:, 0:128].rearrange("p (g d) -> p g d", g=2), eng="act")
                      P.act(gtm[:, tt, :], pc[:, 128:152], AF.Sigmoid)
                  if dbg and blk == 0:
                      dump("d_qT", qT[:], [64, NT, 8, 128], BF16); dump("d_ksT", ksT[:, :, 0:TB], [64, 2, TB], BF16)
                      dump("d_kwT", kwT[:, :, 0:TB], [64, 2, TB], BF16)
                      dump("d_pp", pp[:], [128, 4, TB], BF16); dump("d_hT", hT[:], [128, 8, TB], BF16)
                      dump("d_kcT", kcT[:, 0, :, 0:31], [64, 2, 31], BF16); dump("d_vc1", vc1[0:31, 0, :, :], [31, 2, 65], BF16)
                      dump("d_gtm", gtm[:], [128, NT, 24]); dump("d_vs1", vs1[:, 0:NT, :, :], [128, NT, 2, 65], BF16)
                      dump("d_Ctab", Ctab[:], [64, TB]); dump("d_Stab", Stab[:], [64, TB]); dump("d_mT0", mT[:], [128, 8, TB], BF16)

                  stage(9)
                  bmode[0] = 1
                  pend = []

                  def pv_flush():
                      for fn in pend:
                          fn()
                      del pend[:]

                  for ql in range(NT):
                      qt = QT0 + ql
                      for g in range(2):
                          rhsQ = qT[:, ql, 4 * g:4 * g + 4, :].rearrange("p h q -> p (h q)")
                          Oc, Os, Ow = banks[4], banks[5], banks[6]
                          tiles = []
                          for b in range(blk + 1):
                              n0 = 0 if b == 0 else b * (TB // 16) - 1
                              nbb = TB // 16 - (1 if b == 0 else 0)
                              nk = min(nbb, 8 * qt + 7 - n0)
                              if nk <= 0:
                                  continue
                              tiles.append((b, n0, nk, b == blk))
                          for ti, (b, n0, nk, masked) in enumerate(tiles):
                              ps = bank()
                              P.mm(ps[0:nk, :], kcT[:, b, g, 0:nk], rhsQ, start=True, stop=not masked)
                              if masked:
                                  s0 = 121 - 8 * qt + n0
                                  P.mm(ps[0:nk, :], Jm[:, s0:s0 + nk], cb8[:], start=False, stop=True)
                              P.act(Ec[b][0:nk, :], ps[0:nk, :], AF.Exp, scale=SCALE)
                          for r in range(4):
                              for ti, (b, n0, nk, masked) in enumerate(tiles):
                                  P.mm(Oc[:, r * 128:r * 128 + 65], Ec[b][0:nk, r * 128:(r + 1) * 128], vc1[0:nk, b, g, :],
                                       start=(ti == 0 and r == 0), stop=(ti == len(tiles) - 1 and r == 3))
                          use_sel = qt >= 8
                          if use_sel:
                              pi_ = bank()
                              for r in range(4):
                                  for ti, (b, n0, nk, masked) in enumerate(tiles):
                                      P.mm(pi_[:, r * 64:r * 64 + 33], Ec[b][0:nk, r * 128:(r + 1) * 128], ovT[0:nk, b * 33:(b + 1) * 33],
                                           start=(ti == 0 and r == 0), stop=(ti == len(tiles) - 1 and r == 3))
                              piv = pi_[:, 0:256].rearrange("p (r c) -> p r c", c=64)
                              P.add("dve", lambda e, a=piv: e.reciprocal(rci[:], a[:, :, 32]), reads=[pi_[:, 0:256]], writes=[rci[:]])
                              P.tt(impt[:], piv[:, :, 0:32], rci[:].unsqueeze(2).to_broadcast([128, 4, 32]), ALU.mult)
                              P.add("dve", lambda e: e.tensor_reduce(imp[:], impt[:].rearrange("p r j -> p j r"), AX.X, ALU.add),
                                    reads=[impt[:]], writes=[imp[:]])
                              P.tt(imp2[:], imp[:], Fc[:, qt, :], ALU.max)
                              P.add("dve", lambda e: e.max(out=m8[:, 0:8], in_=imp2[:]), reads=[imp2[:]], writes=[m8[:, 0:8]])
                              P.add("dve", lambda e: e.match_replace(out=imp[:], in_to_replace=m8[:, 0:8], in_values=imp2[:], imm_value=-1e30),
                                    reads=[imp2[:], m8[:, 0:8]], writes=[imp[:]])
                              P.add("dve", lambda e: e.max(out=m8[:, 8:16], in_=imp[:]), reads=[imp[:]], writes=[m8[:, 8:16]])
                              P.ts(negb[:], imp2[:], m8[:, 15:16], ALU.is_lt, -BIG, ALU.mult)
                              pt = bank(); ptb = pt[:].bitcast(BF16)
                              for r in range(4):
                                  P.transpose(ptb[0:32, r * 128:(r + 1) * 128], negb[:], ident[:])
                              P.copy(negbT[:], ptb[0:32, 0:512])
                          k0 = max(0, qt - 4)
                          for kt in range(k0, qt + 1):
                              ps = bank()
                              diag = kt == qt
                              bandt = kt == qt - 4
                              P.mm(ps[:, :], kwT[:, g, kt * 128:(kt + 1) * 128], rhsQ, start=True, stop=not (diag or bandt))
                              if diag:
                                  P.mm(ps[:, :], ident[:], maskb[:, 0, :], start=False, stop=True)
                              if bandt:
                                  P.mm(ps[:, :], ident[:], maskb[:, 1, :], start=False, stop=True)
                              e_ = exp_tile(ps, 128)
                              pv_flush()

                              def pvw(e_=e_, kt=kt, g=g, qt=qt, k0=k0, Ow=Ow):
                                  for r in range(4):
                                      P.mm(Ow[:, r * 128:r * 128 + 65], e_[:, r * 128:(r + 1) * 128], vw1[:, kt, g, :], start=(kt == k0 and r == 0), stop=(kt == qt and r == 3))
                              pend.append(pvw)
                          for kt in range(qt + 1):
                              ps = bank()
                              last_plain = not (use_sel or kt == qt)
                              P.mm(ps[:, :], ksT[:, g, kt * 128:(kt + 1) * 128], rhsQ, start=True, stop=last_plain)
                              if use_sel:
                                  P.mm(ps[:, :], EB[:, kt * 128:(kt + 1) * 128], negbT[:], start=False, stop=(kt != qt))
                              if kt == qt:
                                  P.mm(ps[:, :], ident[:], maskb[:, 0, :], start=False, stop=True)
                              e_ = exp_tile(ps, 128)
                              pv_flush()

                              def pvs(e_=e_, kt=kt, g=g, qt=qt, Os=Os):
                                  for r in range(4):
                                      P.mm(Os[:, r * 128:r * 128 + 65], e_[:, r * 128:(r + 1) * 128], vs1[:, kt, g, :], start=(kt == 0 and r == 0), stop=(kt == qt and r == 3))
                              pend.append(pvs)
                          pv_flush()
                          for bi, Ob in enumerate((Oc, Os, Ow)):
                              ov_ = Ob[:, :].rearrange("p (r c) -> p r c", c=128)
                              P.ts(den[:, bi, :], ov_[:, :, 64], 1e-30, ALU.max)
                          P.add("dve", lambda e: e.reciprocal(den[:], den[:]), reads=[den[:]], writes=[den[:]])
                          gv = gtm[:, ql, 12 * g:12 * g + 12].rearrange("p (h b) -> p b h", b=3)
                          P.tt(fcm[:], den[:], gv, ALU.mult)
                          for bi, Ob in enumerate((Oc, Os, Ow)):
                              ov_ = Ob[:, :].rearrange("p (r c) -> p r c", c=128)
                              fb = fcm[:, bi, :].unsqueeze(2).to_broadcast([128, 4, 64])
                              if bi == 0:
                                  P.tt(oacc[:], ov_[:, :, 0:64], fb, ALU.mult)
                              else:
                                  P.tt(otmp[:], ov_[:, :, 0:64], fb, ALU.mult)
                                  dst = oacc[:] if bi == 1 else Otm[:, g * 256:(g + 1) * 256].rearrange("p (r d) -> p r d", d=64)
                                  P.tt(dst, oacc[:], otmp[:], ALU.add)
                      pt = bank(); ptb = pt[:].bitcast(BF16)
                      for c4 in range(4):
                          P.transpose(ptb[:, c4 * 128:(c4 + 1) * 128], Otm[:, c4 * 128:(c4 + 1) * 128], ident[:])
                      P.copy(OT[:, :, ql * 128:(ql + 1) * 128], ptb[:, 0:512].rearrange("p (c q) -> p c q", c=4), eng="act")
                  if dbg and blk == 0:
                      dump("d_OT", OT[:], [128, 4, TB], BF16)

                  stage(10)
                  bmode[0] = 0
                  for mb in range(2):
                      sln, skn = slot(); slnv = sln[:].rearrange("p a b -> p (a b)").rearrange("p (g c) -> p g c", g=4)
                      wload(slnv, wno.rearrange("(g p) c -> p g c", p=128), skn)
                      sgp, skp = slot(); wload(sgp[:, :, 0:512], w_cols(w_in, C_MG + mb * 512, 512), skp)
                      sgn, skg = slot(); wload(sgn[:, :, 0:512], w_cols(w_in, C_MG + 1024 + mb * 512, 512), skg)
                      for mi in range(4):
                          mc = mb * 4 + mi
                          p_gp = bank(); proj_fm(p_gp, sgp, mi * 128, 128)
                          p_gn = bank(); proj_fm(p_gn, sgn, mi * 128, 128)
                          p_yn = bank()
                          for c4 in range(4):
                              P.mm(p_yn[:, 0:TB], slnv[:, c4, mc * 128:(mc + 1) * 128], OT[:, c4, :], start=(c4 == 0), stop=(c4 == 3))
                          ga = gsg[2 * (mrr[0] % 2)]; gb = gsg[2 * (mrr[0] % 2) + 1]; tf = tmpfs[mrr[0] % 2]
                          mrr[0] += 1
                          P.act(ga[:], p_gp[:, 0:TB], AF.Sigmoid)
                          P.act(gb[:], p_gn[:, 0:TB], AF.Sigmoid)
                          P.tt(tf[:], p_yn[:, 0:TB], gb[:], ALU.mult)
                          P.tt(ga[:], ga[:], mT[:, mc, :], ALU.mult)
                          P.tt(mT[:, mc, :], ga[:], tf[:], ALU.add)
                  sa, ska = slot(); sbb, skb = slot()
                  wo_v = wout.rearrange("(kc p) c -> p kc c", p=128)
                  wload(sa[:, :, :], wo_v[:, :, 0:512], ska); wload(sbb[:, :, :], wo_v[:, :, 512:1024], skb)
                  for tt in range(NT):
                      for hh, sw in enumerate((sa, sbb)):
                          pb = bank()
                          for kc in range(8):
                              P.mm(pb[:, :], mT[:, kc, tt * 128:(tt + 1) * 128], sw[:, kc, :], start=(kc == 0), stop=(kc == 7))
                          P.tt(xres[:, tt, hh * 512:(hh + 1) * 512], xres[:, tt, hh * 512:(hh + 1) * 512], pb[:, :], ALU.add)
                  if dbg and blk == 0:
                      dump("d_x1", xres[:], [128, NT, D])

                  stage(11)
                  norm_T(gffn)
                  fgroups = [(f0, min(4, 22 - f0)) for f0 in range(0, 22, 4)]

                  def ffn_wload(f0, nf):
                      sg_, skg_ = slot(); wload(sg_[:, :, 0:nf * 128], w_cols(wup, f0 * 128, nf * 128), skg_)
                      sv_, skv_ = slot(); wload(sv_[:, :, 0:nf * 128], w_cols(wup, DFF + f0 * 128, nf * 128), skv_)
                      return sg_, sv_

                  cur_w = ffn_wload(*fgroups[0])
                  pre = None
                  for gi_, (f0, nf) in enumerate(fgroups):
                      wload(wdsb[:, 0:nf, :], wdn[f0 * 128:(f0 + nf) * 128, :].rearrange("(fc p) c -> p fc c", p=128), "wd")
                      sg_, sv_ = cur_w
                      for fi in range(nf):
                          f = f0 + fi
                          if fi == 0 and pre is not None:
                              p_g, p_v = pre
                          else:
                              p_g = bank(); proj_fm(p_g, sg_, fi * 128, 128)
                              p_v = bank(); proj_fm(p_v, sv_, fi * 128, 128)
                          gbuf = gbufs[frr[0] % 2]; abuf = abufs[frr[0] % 2]
                          frr[0] += 1
                          P.copy(gbuf[:, 0:2], halo[:, f, :])
                          P.copy(gbuf[:, 2:2 + TB], p_g[:, 0:TB], eng="act")
                          P.copy(halo[:, f, :], gbuf[:, TB:TB + 2])
                          P.ts(abuf[:], gbuf[:, 2:2 + TB], colsb[:, 6, f:f + 1], ALU.mult, convb_sb[:, f:f + 1], ALU.add)
                          P.stt(abuf[:], gbuf[:, 1:1 + TB], colsb[:, 5, f:f + 1], abuf[:], ALU.mult, ALU.add)
                          P.stt(abuf[:], gbuf[:, 0:TB], colsb[:, 4, f:f + 1], abuf[:], ALU.mult, ALU.add)
                          P.act(abuf[:], abuf[:], AF.Silu)
                          P.tt(actT[:, fi, :], abuf[:], p_v[:, 0:TB], ALU.mult)
                      excl = ()
                      pre = None
                      if gi_ + 1 < len(fgroups):
                          cur_w = ffn_wload(*fgroups[gi_ + 1])
                          p_g = bank(); proj_fm(p_g, cur_w[0], 0, 128)
                          p_v = bank(); proj_fm(p_v, cur_w[1], 0, 128)
                          pre = (p_g, p_v)
                          excl = (bank_id(p_g), bank_id(p_v))
                      for tt in range(NT):
                          for hh in range(2):
                              pb = bank(excl)
                              for fi in range(nf):
                                  P.mm(pb[:, :], actT[:, fi, tt * 128:(tt + 1) * 128], wdsb[:, fi, hh * 512:(hh + 1) * 512], start=(fi == 0), stop=(fi == nf - 1))
                              P.tt(xres[:, tt, hh * 512:(hh + 1) * 512], xres[:, tt, hh * 512:(hh + 1) * 512], pb[:, :], ALU.add)
                  for tt in range(NT):
                      P.dma(out[s_i, T0 + tt * 128:T0 + (tt + 1) * 128, :], xres[:, tt, :], key="o", is_out=True)
        except _Stop:
            for tt in range(NT):
                P.dma(out[0, tt * 128:(tt + 1) * 128, :], xres[:, tt, :], key="o", is_out=True)
        P.emit()
    return nc, P, dbg_list


def host_consts():
    c = {}
    c["c_ident"] = np.eye(128, dtype=np.float32)
    k = np.arange(128)[:, None]; q = np.arange(128)[None, :]
    caus = np.where(k <= q, 0.0, -BIG).astype(np.float32)
    band = np.where(k > q, 0.0, -BIG).astype(np.float32)
    c["c_mask"] = np.stack([np.tile(caus, (1, 4)), np.tile(band, (1, 4))], axis=1).astype(np.float32)
    i8 = np.arange(8)[:, None]
    cb = np.where(16 * i8 + 15 <= q, 0.0, -BIG).astype(np.float32)
    c["c_cb8"] = np.tile(cb, (1, 4)).astype(np.float32)
    J = np.zeros((8, 256), np.float32)
    for i in range(8):
        J[i, i + 120] = 1.0
    c["c_J"] = J
    EBm = np.zeros((32, S), np.float32)
    for j in range(32):
        EBm[j, j * 64:(j + 1) * 64] = 1.0
    c["c_EB"] = EBm
    n_cmp = 127
    sel_start = np.arange(32)[:, None] * 64
    cmp_start = np.arange(n_cmp)[None, :] * 16
    ov = np.clip(np.minimum(sel_start + 64, cmp_start + 32) - np.maximum(sel_start, cmp_start), 0, None) / 32.0
    ovfull = np.zeros((n_cmp, 33), np.float32); ovfull[:, 0:32] = ov.T; ovfull[:, 32] = 1.0
    ovt = np.zeros((NBLK, 32, 33), np.float32)
    for b in range(NBLK):
        n0 = 0 if b == 0 else b * (TB // 16) - 1
        nbb = TB // 16 - (1 if b == 0 else 0)
        ovt[b, 0:nbb] = ovfull[n0:n0 + nbb]
    c["c_ov"] = np.ascontiguousarray(ovt.transpose(1, 0, 2).reshape(32, NBLK * 33))
    F = np.zeros((16, 128, 32), np.float32)
    for qt in range(16):
        t = qt * 128 + np.arange(128)
        cur = t // 64
        for j in range(32):
            F[qt, :, j] = np.where((j == 0) | (j == cur) | (j == cur - 1), 1e9, 0.0)
    c["c_F"] = np.ascontiguousarray(F.transpose(1, 0, 2))
    rope = np.zeros((64, 2), np.float32)
    inv = (500000.0 ** (-(np.arange(8, dtype=np.float32) * 2.0 / 16))).astype(np.float32)
    for d in range(16):
        rope[d, 0] = inv[d % 8]
        rope[d, 1] = -1.0 if d < 8 else 1.0
    c["c_rope"] = rope
    pm = np.zeros((64, 64), np.float32)
    for m in range(16):
        pm[m + 8 if m < 8 else m - 8, m] = 1.0
    c["c_perm"] = pm
    invc = np.zeros((128, 4, 16), np.float32)
    for gi, w in enumerate((2, 4, 8, 16)):
        invc[:, gi, :] = 1.0 / np.minimum(np.arange(16) + 1.0, float(w))
    c["c_invc"] = invc
    return c


_CACHE = {}


def kernel(**inputs):
    n = 8
    B = inputs["x"].shape[0]
    nseq = B // n
    if "nc" not in _CACHE:
        _CACHE["nc"] = build(nseq)[0]
    nc = _CACHE["nc"]
    consts = host_consts()
    shared = {}
    for k_, v_ in inputs.items():
        if k_ in ("x", "positions"):
            continue
        a = np.ascontiguousarray(v_)
        shared[k_] = a[0] if a.shape[0] == 1 else a
    in_maps = []
    for c_ in range(n):
        m = dict(shared); m.update(consts)
        m["x"] = np.ascontiguousarray(inputs["x"][c_ * nseq:(c_ + 1) * nseq])
        m["pos"] = np.ascontiguousarray(inputs["positions"][c_ * nseq:(c_ + 1) * nseq]).astype(np.int32)
        in_maps.append(m)
    res = run_bass_kernel_spmd(nc, in_maps, core_ids=list(range(n)))
    return np.concatenate([r["out"] for r in res.results], axis=0).astype(np.float32)
```
